# Optimizing a Trainium2 kernel written in Bass

```python
import jax, jax.numpy as jnp
from jax import lax
import numpy as np

D_MODEL = 2048
BATCH = 16
SEQ = 256
DEPTH = 2
DEC_BATCH = 2
DEC_SEQ = 2048
PAST_LEN = 256

GRID_W = 64
EPS = 1e-6
HEAD_DIM = 128
ATT_Q_HEADS = 8
ATT_KV_HEADS = 2
ATT_GROUP = ATT_Q_HEADS // ATT_KV_HEADS
ATT_WIDTH = ATT_Q_HEADS * HEAD_DIM
KV_WIDTH = ATT_KV_HEADS * HEAD_DIM
Q_BLOCK = 128
ROPE_THETA = 10000.0
ROPE_FREQS = HEAD_DIM // 4
M_HEADS = 4
M_DK = 256
M_DV = 256
M_WIDTH = M_HEADS * M_DV
M_CHUNK = 128
CONV_WIDTH = 1024
CONV_K = 3
N_BRANCH = 3
BRANCH_W = 1024
FF_HIDDEN = ((8 * D_MODEL + 767) // 768) * 256
IN_WIDTH = ATT_WIDTH + 2 * KV_WIDTH + 2 * M_HEADS * M_DK + 2 * M_WIDTH + 4 * M_HEADS + 3 * CONV_WIDTH + N_BRANCH * D_MODEL

kernel_name = 'hybrid_gqa_mlstm_conv_diffusion_step'


def rmsnorm(x, g):
    xf = x.astype(jnp.float32)
    y = xf * lax.rsqrt(jnp.mean(xf * xf, axis=-1, keepdims=True) + EPS)
    return (y * g.astype(jnp.float32)).astype(x.dtype)


def axial_rope(n_tokens):
    rows = n_tokens // GRID_W
    row = jnp.repeat(jnp.arange(rows), GRID_W)
    col = jnp.tile(jnp.arange(GRID_W), rows)
    inv = ROPE_THETA ** (-jnp.arange(ROPE_FREQS, dtype=jnp.float32) / ROPE_FREQS)
    ang = jnp.stack([row, col], axis=-1).astype(jnp.float32)[:, :, None] * inv
    return jnp.cos(ang), jnp.sin(ang)


def apply_rope(x, cos, sin):
    B, T, H, _ = x.shape
    xr = x.reshape(B, T, H, 2, 2, ROPE_FREQS).astype(jnp.float32)
    x1, x2 = xr[..., 0, :], xr[..., 1, :]
    c = cos[None, :, None]
    s = sin[None, :, None]
    out = jnp.stack([x1 * c - x2 * s, x2 * c + x1 * s], axis=-2)
    return out.reshape(x.shape).astype(x.dtype)


def attend(q, k, v):
    B, T = q.shape[0], q.shape[1]
    nb = T // Q_BLOCK
    qb = (q * HEAD_DIM ** -0.5).reshape(B, nb, Q_BLOCK, ATT_KV_HEADS, ATT_GROUP, HEAD_DIM)
    qb = jnp.moveaxis(qb, 1, 0)

    def block(qblk):
        s = jnp.einsum('bqkgd,bskd->bkgqs', qblk, k).astype(jnp.float32)
        p = jax.nn.softmax(s, axis=-1).astype(v.dtype)
        return jnp.einsum('bkgqs,bskd->bqkgd', p, v)

    o = lax.map(block, qb)
    return jnp.moveaxis(o, 0, 1).reshape(B, T, ATT_WIDTH)


def mlstm_scan(q, k, v, ig, lf, C0, n0, m0):
    B, T, H, _ = q.shape
    nc = T // M_CHUNK

    def chunks(a):
        a = a.reshape((B, nc, M_CHUNK, H) + a.shape[3:])
        return jnp.moveaxis(a, (1, 3), (0, 2))

    tril = jnp.tril(jnp.ones((M_CHUNK, M_CHUNK), dtype=bool))

    def step(carry, xs):
        C, n, m = carry
        qc, kc, vc, ic, fc = xs
        b = jnp.cumsum(fc, axis=-1)
        logd = jnp.where(tril, b[..., :, None] - b[..., None, :] + ic[..., None, :], -jnp.inf)
        g = b + m[..., None]
        mt = jnp.maximum(g, jnp.max(logd, axis=-1))
        s = jnp.einsum('bhtd,bhsd->bhts', qc, kc) * jnp.exp(logd - mt[..., None])
        inter = jnp.exp(g - mt)
        num = inter[..., None] * jnp.einsum('bhtd,bhdv->bhtv', qc, C) + jnp.einsum('bhts,bhsv->bhtv', s, vc)
        den = inter * jnp.einsum('bhtd,bhd->bht', qc, n) + jnp.sum(s, axis=-1)
        h = num / jnp.maximum(jnp.abs(den), jnp.exp(-mt))[..., None]
        bl = b[..., -1]
        wl = bl[..., None] - b + ic
        mn = jnp.maximum(bl + m, jnp.max(wl, axis=-1))
        w = jnp.exp(wl - mn[..., None])
        dec = jnp.exp(bl + m - mn)
        Cn = dec[..., None, None] * C + jnp.einsum('bhs,bhsd,bhsv->bhdv', w, kc, vc)
        nn_ = dec[..., None] * n + jnp.einsum('bhs,bhsd->bhd', w, kc)
        return (Cn, nn_, mn), h

    (C, n, m), h = lax.scan(step, (C0, n0, m0), (chunks(q), chunks(k), chunks(v), chunks(ig), chunks(lf)))
    h = jnp.moveaxis(h, (0, 2), (1, 3)).reshape(B, T, H, v.shape[-1])
    return h, (C, n, m)


def conv3(u, w):
    up = jnp.pad(u, ((0, 0), (1, 1), (0, 0)))
    return w[0] * up[:, :-2] + w[1] * up[:, 1:-1] + w[2] * up[:, 2:]


def modulation(cond, w_mod, b_mod):
    mod = jax.nn.silu(cond) @ w_mod + b_mod
    return jnp.split(mod[:, None, :], 6, axis=-1)


def swiglu(h, w_ffn_in, w_ffn_out):
    gate, up = jnp.split(h @ w_ffn_in, 2, axis=-1)
    return (jax.nn.silu(gate) * up) @ w_ffn_out


def mixer(h, ctx_kv, init_f, init_b, rope, w_in, q_norm, k_norm, gate_bias, m_norm, conv_w, w_branch, w_out):
    B, T, _ = h.shape
    sizes = (ATT_WIDTH, KV_WIDTH, KV_WIDTH, M_HEADS * M_DK, M_HEADS * M_DK, M_WIDTH, M_WIDTH,
             4 * M_HEADS, CONV_WIDTH, CONV_WIDTH, CONV_WIDTH, N_BRANCH * D_MODEL)
    (aq, ak, av, mq, mk, mv, mo, mg, cb, cc, cx, gl) = jnp.split(
        h @ w_in, np.cumsum(sizes)[:-1].tolist(), axis=-1)

    q = rmsnorm(aq.reshape(B, T, ATT_Q_HEADS, HEAD_DIM), q_norm)
    k = rmsnorm(ak.reshape(B, T, ATT_KV_HEADS, HEAD_DIM), k_norm)
    v = av.reshape(B, T, ATT_KV_HEADS, HEAD_DIM)
    if ctx_kv is None:
        att = attend(q, k, v)
    else:
        cos, sin = rope
        keys = jnp.concatenate([ctx_kv[0].astype(k.dtype), apply_rope(k, cos, sin)], axis=1)
        vals = jnp.concatenate([ctx_kv[1].astype(v.dtype), v], axis=1)
        att = attend(apply_rope(q, cos, sin), keys, vals)

    f32 = jnp.float32
    gates = (mg.reshape(B, T, 4, M_HEADS) + gate_bias.reshape(4, M_HEADS)).astype(f32)
    mqh = mq.reshape(B, T, M_HEADS, M_DK).astype(f32)
    mkh = mk.reshape(B, T, M_HEADS, M_DK).astype(f32) * M_DK ** -0.5
    mvh = mv.reshape(B, T, M_HEADS, M_DV).astype(f32)
    hf, sf = mlstm_scan(mqh, mkh, mvh, gates[:, :, 0], jax.nn.log_sigmoid(gates[:, :, 1]), *init_f)
    rev = lambda a: jnp.flip(a, axis=1)
    hb, sb = mlstm_scan(rev(mqh), rev(mkh), rev(mvh), rev(gates[:, :, 2]),
                        rev(jax.nn.log_sigmoid(gates[:, :, 3])), *init_b)
    hm = rmsnorm(hf + rev(hb), m_norm.reshape(M_HEADS, M_DV)).reshape(B, T, M_WIDTH).astype(h.dtype)
    mlstm_out = hm * jax.nn.sigmoid(mo)

    conv_out = cb * conv3(cc * cx, conv_w)

    branches = jnp.stack([att, mlstm_out, conv_out], axis=2)
    up = jnp.einsum('btgc,gcd->btgd', branches, w_branch)
    merged = jnp.sum(jax.nn.sigmoid(gl.reshape(B, T, N_BRANCH, D_MODEL)) * up, axis=2)
    return merged @ w_out, k, v, sf, sb


def layer(x, cond, ctx_kv, init_f, init_b, rope, w_mod, b_mod, n_pre1, n_post1, n_pre2, n_post2,
          w_in, q_norm, k_norm, gate_bias, m_norm, conv_w, w_branch, w_out, w_ffn_in, w_ffn_out):
    sh1, sc1, g1, sh2, sc2, g2 = modulation(cond, w_mod, b_mod)
    h = rmsnorm(x, n_pre1) * (1 + sc1) + sh1
    mix, k, v, sf, sb = mixer(h, ctx_kv, init_f, init_b, rope, w_in, q_norm, k_norm, gate_bias,
                              m_norm, conv_w, w_branch, w_out)
    x = x + g1 * rmsnorm(mix, n_post1)
    h = rmsnorm(x, n_pre2) * (1 + sc2) + sh2
    x = x + g2 * rmsnorm(swiglu(h, w_ffn_in, w_ffn_out), n_post2)
    return x, k, v, sf, sb


def setup_inputs(seed: int = 0) -> dict:
    key = jax.random.key(seed)
    ks = jax.random.split(key, 32)
    f32 = jnp.float32
    nrm = lambda k, shape, s: jax.random.normal(k, shape, f32) * s
    D = D_MODEL
    ib = nrm(ks[20], (DEPTH, 2, M_HEADS), 0.1)
    fb = 3.0 + 3.0 * jax.random.uniform(ks[21], (DEPTH, 2, M_HEADS), f32)
    gate_bias = jnp.stack([ib[:, 0], fb[:, 0], ib[:, 1], fb[:, 1]], axis=1).reshape(DEPTH, 4 * M_HEADS)
    return {
        'x_prompt': nrm(ks[0], (BATCH, SEQ, D), 1.0),
        'x_sample': nrm(ks[1], (DEC_BATCH, DEC_SEQ, D), 1.0),
        'cache_k': nrm(ks[2], (DEC_BATCH, DEPTH, PAST_LEN, ATT_KV_HEADS, HEAD_DIM), 1.0),
        'cache_v': nrm(ks[3], (DEC_BATCH, DEPTH, PAST_LEN, ATT_KV_HEADS, HEAD_DIM), 1.0),
        'state_C': nrm(ks[4], (DEC_BATCH, DEPTH, 2, M_HEADS, M_DK, M_DV), 0.05),
        'state_n': nrm(ks[5], (DEC_BATCH, DEPTH, 2, M_HEADS, M_DK), 0.1),
        'state_m': nrm(ks[6], (DEC_BATCH, DEPTH, 2, M_HEADS), 1.0),
        'c': nrm(ks[7], (DEC_BATCH, D), 1.0),
        'c_ctx': nrm(ks[8], (D,), 1.0),
        'w_mod': nrm(ks[9], (DEPTH, D, 6 * D), 0.5 * D ** -0.5),
        'b_mod': nrm(ks[10], (DEPTH, 6 * D), 0.01),
        'norm_pre1': 1.0 + nrm(ks[11], (DEPTH, D), 0.1),
        'norm_post1': 1.0 + nrm(ks[12], (DEPTH, D), 0.1),
        'norm_pre2': 1.0 + nrm(ks[13], (DEPTH, D), 0.1),
        'norm_post2': 1.0 + nrm(ks[14], (DEPTH, D), 0.1),
        'w_in': nrm(ks[15], (DEPTH, D, IN_WIDTH), D ** -0.5),
        'q_norm': 1.0 + nrm(ks[16], (DEPTH, HEAD_DIM), 0.1),
        'k_norm': 1.0 + nrm(ks[17], (DEPTH, HEAD_DIM), 0.1),
        'mlstm_gate_bias': gate_bias,
        'mlstm_norm': 1.0 + nrm(ks[18], (DEPTH, M_WIDTH), 0.1),
        'conv_w': nrm(ks[19], (DEPTH, CONV_K, CONV_WIDTH), CONV_K ** -0.5),
        'w_branch': nrm(ks[22], (DEPTH, N_BRANCH, BRANCH_W, D), BRANCH_W ** -0.5),
        'w_out': nrm(ks[23], (DEPTH, D, D), D ** -0.5),
        'w_ffn_in': nrm(ks[24], (DEPTH, D, 2 * FF_HIDDEN), D ** -0.5),
        'w_ffn_out': nrm(ks[25], (DEPTH, FF_HIDDEN, D), FF_HIDDEN ** -0.5),
    }


def reference(x_prompt, x_sample, cache_k, cache_v, state_C, state_n, state_m, c, c_ctx,
              w_mod, b_mod, norm_pre1, norm_post1, norm_pre2, norm_post2, w_in, q_norm, k_norm,
              mlstm_gate_bias, mlstm_norm, conv_w, w_branch, w_out, w_ffn_in, w_ffn_out):
    f32 = jnp.float32
    nb = x_prompt.shape[0]
    zero = (jnp.zeros((nb, M_HEADS, M_DK, M_DV), f32), jnp.zeros((nb, M_HEADS, M_DK), f32),
            jnp.zeros((nb, M_HEADS), f32))
    rope = axial_rope(x_sample.shape[1])
    xp, xs = x_prompt, x_sample
    ks_, vs_, Cs, ns, ms = [], [], [], [], []
    for l in range(DEPTH):
        lw = (w_mod[l], b_mod[l], norm_pre1[l], norm_post1[l], norm_pre2[l], norm_post2[l],
              w_in[l], q_norm[l], k_norm[l], mlstm_gate_bias[l], mlstm_norm[l], conv_w[l],
              w_branch[l], w_out[l], w_ffn_in[l], w_ffn_out[l])
        xp, k, v, sf, sb = layer(xp, c_ctx[None], None, zero, zero, None, *lw)
        ks_.append(k)
        vs_.append(v)
        Cs.append(jnp.stack([sf[0], sb[0]], axis=1))
        ns.append(jnp.stack([sf[1], sb[1]], axis=1))
        ms.append(jnp.stack([sf[2], sb[2]], axis=1))
        init_f = (state_C[:, l, 0].astype(f32), state_n[:, l, 0].astype(f32), state_m[:, l, 0].astype(f32))
        init_b = (state_C[:, l, 1].astype(f32), state_n[:, l, 1].astype(f32), state_m[:, l, 1].astype(f32))
        xs, _, _, _, _ = layer(xs, c, (cache_k[:, l], cache_v[:, l]), init_f, init_b, rope, *lw)
    new_cache_k = jnp.stack(ks_, axis=1)
    new_cache_v = jnp.stack(vs_, axis=1)
    new_state_C = jnp.stack(Cs, axis=1)
    new_state_n = jnp.stack(ns, axis=1)
    new_state_m = jnp.stack(ms, axis=1)
    return (xp, xs, new_cache_k, new_cache_v, new_state_C, new_state_n, new_state_m)
```

```python
import contextlib
import numpy as np
import concourse.bass as bass
import concourse.mybir as mybir
from concourse.bass_utils import run_bass_kernel_spmd

F32 = mybir.dt.float32
BF16 = mybir.dt.bfloat16
ALU = mybir.AluOpType
AF = mybir.ActivationFunctionType
AX = mybir.AxisListType

STREAMS = ("sp", "act", "dve", "pool", "pe")

D = 2048
DEPTH = 2
TS = 2048
TP = 512
NCTX = 256
FF = 5632
INW = 14864
SEC = dict(aq=0, ak=1024, av=1280, mq=1536, mk=2560, mv=3584, mo=4608, mg=5632,
           cb=5648, cc=6672, cx=7696, gl=8720)
EPS = 1e-6
WC = 256
WK = 8
NEG = -1.0e30


class Prog:
    def __init__(self, nc):
        self.nc = nc
        self.gstack = contextlib.ExitStack()
        self.NSLOT = 16
        self.dma_i = {"sp": 0, "act": 0, "pool": 0}
        self.sem_names = ["c_act", "c_dve", "c_pool", "c_pe"] + ["d_sp%d" % k for k in range(self.NSLOT)]
        self.sems = {n: self.gstack.enter_context(nc.semaphore(n)) for n in self.sem_names}
        self.cnt = {n: 0 for n in self.sem_names}
        self.seen = {s: {} for s in STREAMS}
        self.lastw = {}
        self.readers = {}
        self.ops = {s: [] for s in STREAMS}
        self.pstack = None
        self.n_ops = 0
        self.emitted = {n: 0 for n in self.sem_names}
        self.dbg = None

    def gsb(self, name, shape, dt):
        return self.gstack.enter_context(self.nc.sbuf_tensor(name, list(shape), dt))

    def gps(self, name, shape, dt):
        return self.gstack.enter_context(self.nc.psum_tensor(name, list(shape), dt))

    def begin(self):
        self.pstack = contextlib.ExitStack()

    def sb(self, name, shape, dt):
        self.uid = getattr(self, "uid", 0) + 1
        return self.pstack.enter_context(self.nc.sbuf_tensor("%s_u%d" % (name, self.uid), list(shape), dt))

    @staticmethod
    def _flat(keys):
        out = []
        for k in keys:
            if isinstance(k, (list, tuple)):
                out.extend(Prog._flat(k))
            else:
                out.append(k)
        return out

    def _deps(self, stream, sem_self, reads, writes):
        deps = {}
        def add(d):
            if d is not None and deps.get(d[0], 0) < d[1]:
                deps[d[0]] = d[1]
        for k in reads:
            add(self.lastw.get(k))
        for k in writes:
            add(self.lastw.get(k))
            for d in self.readers.get(k, ()):
                add(d)
        waits = []
        seen = self.seen[stream]
        for s, c in deps.items():
            if s == sem_self and stream == "pe":
                continue
            if seen.get(s, 0) < c:
                seen[s] = c
                waits.append((s, c))
        return waits

    def _commit(self, sem, reads, writes):
        self.cnt[sem] += 1
        tag = (sem, self.cnt[sem])
        for k in reads:
            self.readers.setdefault(k, []).append(tag)
        for k in writes:
            self.lastw[k] = tag
            self.readers[k] = []

    class _Rec:
        def __init__(self):
            self.calls = []

        def __getattr__(self, name):
            def f(*a, **k):
                import sys
                self.calls.append((name, a, k, sys._getframe(1).f_lineno))
                return None
            return f

    def op(self, stream, fn, reads=(), writes=()):
        reads, writes = self._flat(reads), self._flat(writes)
        sem = "c_" + stream
        rec = Prog._Rec()
        fn(rec)
        calls = rec.calls
        assert calls
        waits = self._deps(stream, sem, reads, writes)
        if stream == "pe":
            self.ops[stream].append((calls, waits, sem, "last"))
            self._commit(sem, reads, writes)
        else:
            self.ops[stream].append((calls, waits, sem, "each"))
            n = len(calls)
            self.cnt[sem] += n - 1
            self._commit(sem, reads, writes)
            self.seen[stream][sem] = max(self.seen[stream].get(sem, 0), self.cnt[sem] - 1)
        self.n_ops += 1

    def dma(self, stream, out, in_, reads=(), writes=(), **kw):
        reads, writes = self._flat(reads), self._flat(writes)
        stream = "sp"
        i = self.dma_i[stream]
        self.dma_i[stream] += 1
        sem = "d_%s%d" % (stream, i % self.NSLOT)
        waits = self._deps(stream, sem, reads, writes)
        prev = self.cnt[sem]
        if prev > 0 and self.seen[stream].get(sem, 0) < prev:
            self.seen[stream][sem] = prev
            waits.append((sem, prev))
        self.ops[stream].append(([("dma_start", (), dict(out=out, in_=in_, **kw), 0)], waits, sem, "dma"))
        self._commit(sem, reads, writes)
        self.n_ops += 1

    def barrier(self):
        for s in STREAMS:
            waits = []
            for n in self.sem_names:
                c = self.cnt[n]
                if self.seen[s].get(n, 0) < c:
                    self.seen[s][n] = c
                    waits.append((n, c))
            self.ops[s].append((None, waits, None, None))

    def end(self):
        self.barrier()
        prog = self
        nc = self.nc
        if self.dbg is not None:
            print("phase end: sbuf remaining", nc.sbuf_bytes_remaining)
        def run(stream, eng):
            base = dict(prog.emitted)
            for calls, waits, sem, mode in prog.ops[stream]:
                for s_, c in waits:
                    eng.wait_ge(prog.sems[s_], c * (16 if s_[0] == "d" else 1))
                if calls is None:
                    continue
                n = len(calls)
                for i, (name, a, k, lineno) in enumerate(calls):
                    if mode == "each" and i > 0:
                        eng.wait_ge(prog.sems[sem], prog.emitted[sem])
                    ins = getattr(eng, name)(*a, **k)
                    if prog.dbg is not None:
                        prog.dbg.append((str(ins), lineno))
                    if mode == "each":
                        ins.then_inc(prog.sems[sem], 1)
                        prog.emitted[sem] += 1
                    elif mode == "last":
                        if i == n - 1:
                            ins.then_inc(prog.sems[sem], 1)
                            prog.emitted[sem] += 1
                    else:
                        ins.then_inc(prog.sems[sem], 16)
                        prog.emitted[sem] += 1
        with nc.Block() as block:
            @block.sync
            def _(e):
                run("sp", e)
            @block.scalar
            def _(e):
                run("act", e)
            @block.vector
            def _(e):
                run("dve", e)
            @block.gpsimd
            def _(e):
                run("pool", e)
            @block.tensor
            def _(e):
                run("pe", e)
        self.ops = {s: [] for s in STREAMS}
        self.pstack.close()
        self.pstack = None

    def close(self):
        if self.dbg is not None:
            print("final sem counts", self.cnt)
        self.gstack.close()


def build_program(debug=False, stop=None, only_job=None, depth=DEPTH, dump=(), stop_l=0, start_l=0):
    nc = bass.Bass("TRN2", target_bir_lowering=False)

    def din(name, shape):
        return nc.dram_tensor(name, list(shape), F32, kind="ExternalInput").ap()

    def dout(name, shape):
        return nc.dram_tensor(name, list(shape), F32, kind="ExternalOutput").ap()

    def dscr(name, shape, dt):
        if name in dump:
            return nc.dram_tensor(name, list(shape), dt, kind="ExternalOutput").ap()
        return nc.dram_tensor(name, list(shape), dt).ap()

    x_s = din("x_s", [TS, D])
    x_p = din("x_p", [TP, D])
    cache_k = din("cache_k", [DEPTH, NCTX, 256])
    cache_v = din("cache_v", [DEPTH, NCTX, 256])
    state_C = din("state_C", [DEPTH, 2, 4, 256, 256])
    state_n = din("state_n", [DEPTH, 2, 4, 256])
    state_m = din("state_m", [DEPTH, 8])
    cond = din("cond", [2, D])
    w_mod = din("w_mod", [DEPTH, D, 6 * D])
    b_mod = din("b_mod", [DEPTH, 6 * D])
    n_pre1 = din("norm_pre1", [DEPTH, D])
    n_post1 = din("norm_post1", [DEPTH, D])
    n_pre2 = din("norm_pre2", [DEPTH, D])
    n_post2 = din("norm_post2", [DEPTH, D])
    w_in = din("w_in", [DEPTH, D, INW])
    q_norm = din("q_norm", [DEPTH, 128])
    k_norm = din("k_norm", [DEPTH, 128])
    gate_bias = din("mlstm_gate_bias", [DEPTH, 16])
    m_norm = din("mlstm_norm", [DEPTH, 1024])
    conv_w = din("conv_w", [DEPTH, 3, 1024])
    w_branch = din("w_branch", [DEPTH, 3, 1024, D])
    w_out = din("w_out", [DEPTH, D, D])
    w_ffn_in = din("w_ffn_in", [DEPTH, D, 2 * FF])
    w_ffn_out = din("w_ffn_out", [DEPTH, FF, D])
    cosT = din("cosT", [TS + TP, 64])
    sinT = din("sinT", [TS + TP, 64])
    cmask = din("cmask", [4, 128, 128])

    y_s = dout("y_s", [TS, D])
    y_p = dout("y_p", [TP, D])
    o_k = dout("o_k", [DEPTH, TP, 256])
    o_v = dout("o_v", [DEPTH, TP, 256])
    o_C = dout("o_C", [2, DEPTH, 2, 4, 256, 256])
    o_n = dout("o_n", [2, DEPTH, 2, 4, 256])
    o_m = dout("o_m", [2, DEPTH, 8])

    MOD = dscr("MOD", [DEPTH, 2, 6 * D], F32)
    XS1 = dscr("XS1", [TS, D], F32)
    XP1 = dscr("XP1", [TP, D], F32)
    XMID = dscr("XMID", [TS, D], F32)
    MODC = dscr("MODC", [3, D], F32)
    JK = dscr("JK", [TS, 256], F32)
    JV = dscr("JV", [TS, 256], F32)
    JC = dscr("JC", [2, 4, 256, 256], F32)
    JN = dscr("JN", [2, 4, 256], F32)
    JM = dscr("JM", [8], F32)
    NKS = NCTX + TS
    QT = dscr("QT", [TS // 128, 128, 1024], BF16)
    KT = dscr("KT", [128, 2, NKS], BF16)
    VV = dscr("VV", [NKS, 256], BF16)
    MQT = dscr("MQT", [TS // 128, 128, 1024], BF16)
    MKT = dscr("MKT", [TS // 128, 128, 1024], BF16)
    MK = dscr("MK", [TS, 1024], BF16)
    MV = dscr("MV", [TS, 1024], BF16)
    MOS = dscr("MOS", [TS, 1024], BF16)
    MG = dscr("MG", [TS, 16], F32)
    CB = dscr("CB", [128, 8, TS], BF16)
    CU = dscr("CU", [128, 8, TS], BF16)
    GS = dscr("GS", [128, 48, TS], BF16)
    ATT = dscr("ATT", [128, 8, TS], BF16)
    MLT = dscr("MLT", [128, 8, TS], BF16)

    P = Prog(nc)
    if debug:
        P.dbg = []
        nc._dbg = P.dbg
    ident = P.gsb("ident", [128, 128], BF16)
    identf = P.gsb("identf", [128, 128], F32)
    onesf = P.gsb("onesf", [128, 128], F32)
    onesb = P.gsb("onesb", [128, 128], BF16)
    cm = P.gsb("cm", [128, 4, 128], F32)
    pb = [P.gps("pb%d" % i, [128, 512], F32) for i in range(7)]
    ptb = P.gps("ptb", [128, 1024], BF16)
    PBK = [["pb%da" % i, "pb%db" % i] for i in range(7)]

    wstate = {"i": 0}

    class WPool:
        def __init__(self):
            self.st = [P.sb("wst%d" % i, [128, WK, WC], F32) for i in range(2)]
            self.bf = [P.sb("wbf%d" % i, [128, WK, WC], BF16) for i in range(3)]
            self.i = 0

        def block(self, src, k0, nk, c0, ncols):
            i = self.i
            self.i += 1
            st = self.st[i % 2]
            bf = self.bf[i % 3]
            ks, kb = "wst%d" % (i % 2), "wbf%d" % (i % 3)
            P.dma("sp", st[:, 0:nk, 0:ncols],
                  src[k0 * 128:(k0 + nk) * 128, c0:c0 + ncols].rearrange("(kc p) n -> p kc n", p=128),
                  writes=[ks])
            P.op("pool", lambda e, st=st, bf=bf, nk=nk, ncols=ncols: e.tensor_copy(out=bf[:, 0:nk, 0:ncols], in_=st[:, 0:nk, 0:ncols]),
                 reads=[ks], writes=[kb])
            return bf, kb

    def lin_fm(W, src, KCtot, c0, ncols, rhs, rhs_key, ntok, consume, wp, bank0=0):
        nblk = (ncols + WC - 1) // WC
        for bi in range(nblk):
            cc0 = c0 + bi * WC
            ncb = min(WC, c0 + ncols - cc0)
            nm = (ncb + 127) // 128
            banks = [bank0 + (2 * (bi % 2) + m) for m in range(nm)]
            nkh = (KCtot + WK - 1) // WK
            for kh in range(nkh):
                nk = min(WK, KCtot - kh * WK)
                bf, kb = wp.block(src, kh * WK, nk, cc0, ncb)
                for m in range(nm):
                    mw = min(128, ncb - m * 128)
                    bk = banks[m]
                    def f(e, bf=bf, kh=kh, nk=nk, m=m, mw=mw, bk=bk, nkh=nkh):
                        ins = None
                        for kc in range(nk):
                            ins = e.matmul(pb[bk][0:mw, 0:ntok], lhsT=bf[:, kc, m * 128:m * 128 + mw],
                                           rhs=rhs[:, kh * WK + kc, 0:ntok],
                                           start=(kh == 0 and kc == 0), stop=(kh == nkh - 1 and kc == nk - 1))
                        return ins
                    P.op("pe", f, reads=[kb, rhs_key], writes=PBK[bk])
            for m in range(nm):
                mw = min(128, ncb - m * 128)
                consume((cc0 - c0) // 128 + m, pb[banks[m]][0:mw, 0:ntok], PBK[banks[m]])

    def lin_tm(W, src, KCtot, c0, ncols, lhs, lhs_key, ntb, consume, wp):
        nblk = (ncols + WC - 1) // WC
        for bi in range(nblk):
            cc0 = c0 + bi * WC
            ncb = min(WC, c0 + ncols - cc0)
            nkh = (KCtot + WK - 1) // WK
            half = (bi % 2) * 256
            for kh in range(nkh):
                nk = min(WK, KCtot - kh * WK)
                bf, kb = wp.block(src, kh * WK, nk, cc0, ncb)
                for tb in range(ntb):
                    def f(e, bf=bf, kh=kh, nk=nk, tb=tb, ncb=ncb, half=half, nkh=nkh):
                        ins = None
                        for kc in range(nk):
                            ins = e.matmul(pb[tb][:, half:half + ncb], lhsT=lhs[:, kh * WK + kc, tb * 128:(tb + 1) * 128],
                                           rhs=bf[:, kc, 0:ncb],
                                           start=(kh == 0 and kc == 0), stop=(kh == nkh - 1 and kc == nk - 1))
                        return ins
                    P.op("pe", f, reads=[kb, lhs_key], writes=[PBK[tb][0 if half == 0 else 1]])
            for tb in range(ntb):
                consume(tb, cc0 - c0, ncb, pb[tb][:, half:half + ncb], [PBK[tb][0 if half == 0 else 1]])

    P.begin()
    def mk_ident(e):
        e.memset(identf[:], 0.0)
        return e.affine_select(out=identf[:], in_=identf[:], pattern=[[-1, 128]], compare_op=ALU.not_equal,
                               fill=1.0, base=0, channel_multiplier=1)
    P.op("pool", mk_ident, writes=["identf"])
    P.op("pool", lambda e: e.tensor_copy(out=ident[:], in_=identf[:]), reads=["identf"], writes=["ident"])
    P.op("pool", lambda e: e.memset(onesf[:], 1.0), writes=["onesf"])
    P.op("pool", lambda e: e.memset(onesb[:], 1.0), writes=["onesb"])
    P.dma("sp", cm[:], cmask.rearrange("a p n -> p a n"), writes=["cm"])
    maskF, maskB, triF, triB = cm[:, 0, :], cm[:, 1, :], cm[:, 2, :], cm[:, 3, :]

    wp = WPool()
    cnd = P.sb("cnd", [128, 2, 16], F32)
    cndb = P.sb("cndb", [128, 16, 2], BF16)
    for ci in range(2):
        P.dma("act", cnd[:, ci, :], cond[ci].rearrange("(kc p) -> p kc", p=128), writes=["cnd"],
              allow_slow_non_contiguous=True)
    P.op("act", lambda e: e.activation(out=cndb[:].rearrange("p k c -> p c k"), in_=cnd[:], func=AF.Silu),
         reads=["cnd"], writes=["cndb"])
    bm = [P.sb("bm%d" % i, [2, WC], F32) for i in range(2)]
    mo_ = [P.sb("mo%d" % i, [2, WC], F32) for i in range(2)]
    for l in range(DEPTH):
        nblk = 6 * D // WC
        for bi in range(nblk):
            c0 = bi * WC
            j = bi % 2
            P.dma("act", bm[j][:], b_mod[l:l + 1, c0:c0 + WC].partition_broadcast(2) if False else
                  b_mod[l, c0:c0 + WC].partition_broadcast(2), writes=["bm%d" % j])
            bk = 4 + j
            for kh in range(2):
                bf, kb = wp.block(w_mod[l], kh * WK, WK, c0, WC)
                def f(e, bf=bf, kh=kh, bk=bk):
                    ins = None
                    for kc in range(WK):
                        ins = e.matmul(pb[bk][0:2, 0:WC], lhsT=cndb[:, kh * WK + kc, :], rhs=bf[:, kc, :],
                                       start=(kh == 0 and kc == 0), stop=(kh == 1 and kc == WK - 1))
                    return ins
                P.op("pe", f, reads=[kb, "cndb"], writes=PBK[bk])
            P.op("dve", lambda e, j=j, bk=bk: e.tensor_tensor(out=mo_[j][:], in0=pb[bk][0:2, 0:WC], in1=bm[j][:], op=ALU.add),
                 reads=[PBK[bk], "bm%d" % j], writes=["mo%d" % j])
            P.dma("act", MOD[l, :, c0:c0 + WC], mo_[j][:], reads=["mo%d" % j], writes=["MOD"])
    P.end()

    def load_row_bc(dst, dst_key, row_ap, stream="act", extra_reads=()):
        P.dma(stream, dst, row_ap.partition_broadcast(128), reads=list(extra_reads), writes=[dst_key])

    def act_rstd(e, rstd, ss, n):
        e.activation(out=rstd, in_=ss, func=AF.Ln, scale=1.0 / n, bias=EPS)
        return e.activation(out=rstd, in_=rstd, func=AF.Exp, scale=-0.5)

    jobs = [
        dict(name="S", T=TS, ci=1, seqs=[(0, TS)], nctx=NCTX, x_in=x_s, x_l1=XS1, y=y_s, rope0=0),
        dict(name="P", T=TP, ci=0, seqs=[(0, 256), (256, 256)], nctx=0, x_in=x_p, x_l1=XP1, y=y_p, rope0=TS),
    ]

    if stop == "0":
        P.close()
        return nc
    for l in range(start_l, depth):
        for job in jobs:
            if only_job is not None and job["name"] != only_job:
                continue
            T = job["T"]
            ci = job["ci"]
            ntile = T // 512
            x_src = job["x_in"] if l == 0 else job["x_l1"]
            x_dst = job["x_l1"] if l == 0 else job["y"]
            isP = job["name"] == "P"
            nctx = job["nctx"]
            NK = nctx + T
            P.begin()
            wp = WPool()
            A1 = P.sb("A1", [128, D], F32)
            B1 = P.sb("B1", [128, D], F32)
            load_row_bc(A1[:], "A1", MOD[l, ci, 1 * D:2 * D], extra_reads=["MOD"])
            load_row_bc(B1[:], "B1", n_pre1[l])
            P.op("dve", lambda e: e.scalar_tensor_tensor(out=A1[:], in0=A1[:], scalar=1.0, in1=B1[:], op0=ALU.add, op1=ALU.mult),
                 reads=["A1", "B1"], writes=["A1"])
            load_row_bc(B1[:], "B1", MOD[l, ci, 0:D], extra_reads=["MOD"])
            gq = P.sb("gq", [128, 128], F32)
            gk = P.sb("gk", [128, 128], F32)
            gb = P.sb("gb", [128, 16], F32)
            load_row_bc(gq[:], "gq", q_norm[l])
            load_row_bc(gk[:], "gk", k_norm[l])
            load_row_bc(gb[:], "gb", gate_bias[l])
            P.op("dve", lambda e: e.tensor_scalar(out=gq[:], in0=gq[:], scalar1=128.0 ** -0.5, scalar2=None, op0=ALU.mult),
                 reads=["gq"], writes=["gq"])
            xs = [P.sb("xs%d" % i, [128, D], F32) for i in range(2)]
            htm = [P.sb("htm%d" % i, [128, D], BF16) for i in range(2)]
            junk = P.sb("junk", [128, D], BF16)
            tmpf = P.sb("tmpf", [128, D], F32)
            ss = P.sb("ss", [128, 8], F32)
            hT = P.sb("hT", [128, 16, 512], BF16)
            cs = P.sb("cs", [128, 4, 64], F32)
            sn_ = P.sb("sn", [128, 4, 64], F32)
            ev = [P.sb("ev%d" % i, [128, WC], F32) for i in range(2)]
            ev2 = [P.sb("evb%d" % i, [128, WC], F32) for i in range(2)]
            evr = [P.sb("evr%d" % i, [128, WC], BF16) for i in range(2)]
            r1 = P.sb("r1", [128, WC], F32)
            r2 = P.sb("r2", [128, WC], F32)
            s4 = P.sb("s4", [128, 4], F32)
            obf = [P.sb("obf%d" % i, [128, 512], BF16) for i in range(3)]
            og = [P.sb("og%d" % i, [128, 16], F32) for i in range(2)]
            QTs = P.sb("QTs", [128, 4, 1024], BF16)
            KTs = P.sb("KTs", [128, 2, 512], BF16)
            MQs = P.sb("MQs", [128, 4, 1024], BF16)
            MKs = P.sb("MKs", [128, 4, 1024], BF16)
            evc = {"i": 0}
            ccs_t = [P.sb("ccs%d" % i, [128, 2, 512], F32) for i in range(2)]

            if nctx:
                for kb_ in range(nctx // 128):
                    t = P.sb("ck%d" % kb_, [128, 256], F32)
                    tb_ = P.sb("ckb%d" % kb_, [128, 256], BF16)
                    tv = P.sb("cv%d" % kb_, [128, 256], F32)
                    tvb = P.sb("cvb%d" % kb_, [128, 256], BF16)
                    kT_ = P.sb("ckT%d" % kb_, [128, 2, 128], BF16)
                    P.dma("act", t[:], cache_k[l, kb_ * 128:(kb_ + 1) * 128, :], writes=["ck%d" % kb_])
                    P.dma("act", tv[:], cache_v[l, kb_ * 128:(kb_ + 1) * 128, :], writes=["cv%d" % kb_])
                    P.op("dve", lambda e, t=t, tb_=tb_: e.tensor_copy(out=tb_[:], in_=t[:]), reads=["ck%d" % kb_], writes=["ckb%d" % kb_])
                    P.op("dve", lambda e, tv=tv, tvb=tvb: e.tensor_copy(out=tvb[:], in_=tv[:]), reads=["cv%d" % kb_], writes=["cvb%d" % kb_])
                    def ftr(e, tb_=tb_):
                        ins = None
                        for h in range(2):
                            ins = e.transpose(ptb[:, h * 128:(h + 1) * 128], tb_[:, h * 128:(h + 1) * 128], ident[:])
                        return ins
                    P.op("pe", ftr, reads=["ckb%d" % kb_, "ident"], writes=["ptb"])
                    P.op("act", lambda e, kT_=kT_: e.copy(out=kT_[:].rearrange("p h t -> p (h t)"), in_=ptb[:, 0:256]),
                         reads=["ptb"], writes=["ckT%d" % kb_])
                    P.dma("act", KT[:, :, kb_ * 128:(kb_ + 1) * 128], kT_[:], reads=["ckT%d" % kb_], writes=["KT"])
                    P.dma("act", VV[kb_ * 128:(kb_ + 1) * 128, :], tvb[:], reads=["cvb%d" % kb_], writes=["VV"])

            for ti in range(ntile):
                t0 = ti * 512
                P.dma("act", cs[:], cosT[job["rope0"] + t0:job["rope0"] + t0 + 512, :].rearrange("(tb p) f -> p tb f", p=128), writes=["cs"])
                P.dma("act", sn_[:], sinT[job["rope0"] + t0:job["rope0"] + t0 + 512, :].rearrange("(tb p) f -> p tb f", p=128), writes=["sn"])
                for tb in range(4):
                    j = tb % 2
                    xk, hk = "xs%d" % j, "htm%d" % j
                    P.dma("sp", xs[j][:], x_src[t0 + tb * 128:t0 + (tb + 1) * 128, :], writes=[xk])
                    def fsq(e, j=j, tb=tb):
                        e.memzero(ss[:, tb:tb + 1])
                        e.activation(out=junk[:], in_=xs[j][:], func=AF.Square, accum_out=ss[:, tb:tb + 1])
                        return act_rstd(e, ss[:, 4 + tb:5 + tb], ss[:, tb:tb + 1], D)
                    P.op("act", fsq, reads=[xk], writes=["junk", "ss%d" % tb])
                    def fn1(e, j=j, tb=tb):
                        e.scalar_tensor_tensor(out=tmpf[:], in0=xs[j][:], scalar=ss[:, 4 + tb:5 + tb], in1=A1[:], op0=ALU.mult, op1=ALU.mult)
                        return e.tensor_tensor(out=htm[j][:], in0=tmpf[:], in1=B1[:], op=ALU.add)
                    P.op("dve", fn1, reads=[xk, "ss%d" % tb, "A1", "B1"], writes=["tmpf", hk])
                    for g in range(2):
                        def ftr(e, j=j, g=g):
                            ins = None
                            for kk in range(8):
                                kc = g * 8 + kk
                                ins = e.transpose(ptb[:, kk * 128:(kk + 1) * 128], htm[j][:, kc * 128:(kc + 1) * 128], ident[:])
                            return ins
                        P.op("pe", ftr, reads=[hk, "ident"], writes=["ptb"])
                        P.op("act", lambda e, g=g, tb=tb: e.copy(out=hT[:, g * 8:(g + 1) * 8, tb * 128:(tb + 1) * 128],
                                                                 in_=ptb[:].rearrange("p (k t) -> p k t", k=8)),
                             reads=["ptb"], writes=["hT"])

                def nxt():
                    evc["i"] += 1
                    return evc["i"] % 2

                def qk_consume(kind):
                    def consume(tb, coff, ncb, ps, pkey):
                        nh = ncb // 128
                        j = nxt()
                        g = gq if kind == "q" else gk
                        gkey = "gq" if kind == "q" else "gk"
                        def f0(e, j=j):
                            e.tensor_copy(out=ev[j][:, 0:ncb], in_=ps)
                            e.tensor_tensor(out=r1[:, 0:ncb], in0=ev[j][:, 0:ncb], in1=ev[j][:, 0:ncb], op=ALU.mult)
                            return e.tensor_reduce(out=s4[:, 0:nh], in_=r1[:, 0:ncb].rearrange("p (h d) -> p h d", h=nh), axis=AX.X, op=ALU.add)
                        P.op("dve", f0, reads=[pkey], writes=["r1", "s4", "ev%d" % j])
                        P.op("act", lambda e: act_rstd(e, s4[:, 0:nh], s4[:, 0:nh], 128), reads=["s4"], writes=["s4"])
                        def f(e, j=j):
                            e.tensor_tensor(out=ev[j][:, 0:ncb].rearrange("p (h d) -> p h d", h=nh), in0=ev[j][:, 0:ncb].rearrange("p (h d) -> p h d", h=nh),
                                            in1=s4[:, 0:nh].unsqueeze(2).to_broadcast([128, nh, 128]), op=ALU.mult)
                            return e.tensor_tensor(out=ev[j][:, 0:ncb].rearrange("p (h d) -> p h d", h=nh), in0=ev[j][:, 0:ncb].rearrange("p (h d) -> p h d", h=nh),
                                                   in1=g[:].unsqueeze(1).to_broadcast([128, nh, 128]), op=ALU.mult)
                        P.op("dve", f, reads=["s4", gkey, "ev%d" % j], writes=["ev%d" % j])
                        if kind == "k":
                            dstk = (o_k[l] if isP else JK)
                            P.dma("act", dstk[t0 + tb * 128:t0 + (tb + 1) * 128, :], ev[j][:, 0:256], reads=["ev%d" % j], writes=["o_k"])
                        def frope(e, j=j, tb=tb):
                            xv = ev[j][:, 0:ncb].rearrange("p (h a x f) -> p h a x f", h=nh, a=2, x=2)
                            ov = evr[j][:, 0:ncb].rearrange("p (h a x f) -> p h a x f", h=nh, a=2, x=2)
                            t1v = r1[:, 0:ncb // 2].rearrange("p (h a f) -> p h a f", h=nh, a=2)
                            t2v = r2[:, 0:ncb // 2].rearrange("p (h a f) -> p h a f", h=nh, a=2)
                            cb_ = cs[:, tb, :].rearrange("p (a f) -> p a f", a=2).unsqueeze(1).to_broadcast([128, nh, 2, 32])
                            sb_ = sn_[:, tb, :].rearrange("p (a f) -> p a f", a=2).unsqueeze(1).to_broadcast([128, nh, 2, 32])
                            x1, x2 = xv[:, :, :, 0, :], xv[:, :, :, 1, :]
                            e.tensor_tensor(out=t1v, in0=x1, in1=cb_, op=ALU.mult)
                            e.tensor_tensor(out=t2v, in0=x2, in1=sb_, op=ALU.mult)
                            e.tensor_tensor(out=ov[:, :, :, 0, :], in0=t1v, in1=t2v, op=ALU.subtract)
                            e.tensor_tensor(out=t1v, in0=x2, in1=cb_, op=ALU.mult)
                            e.tensor_tensor(out=t2v, in0=x1, in1=sb_, op=ALU.mult)
                            return e.tensor_tensor(out=ov[:, :, :, 1, :], in0=t1v, in1=t2v, op=ALU.add)
                        P.op("dve", frope, reads=["ev%d" % j, "cs", "sn"], writes=["r1", "r2", "evr%d" % j])
                        def ftr(e, j=j):
                            ins = None
                            for h in range(nh):
                                ins = e.transpose(ptb[:, h * 128:(h + 1) * 128], evr[j][:, h * 128:(h + 1) * 128], ident[:])
                            return ins
                        P.op("pe", ftr, reads=["evr%d" % j, "ident"], writes=["ptb"])
                        if kind == "q":
                            h0 = coff // 128
                            P.op("act", lambda e, tb=tb, h0=h0: e.copy(out=QTs[:, tb, h0 * 128:h0 * 128 + ncb], in_=ptb[:, 0:ncb]),
                                 reads=["ptb"], writes=["QTs"])
                        else:
                            P.op("act", lambda e, tb=tb: e.copy(out=KTs[:, :, tb * 128:(tb + 1) * 128], in_=ptb[:, 0:256].rearrange("p (h t) -> p h t", h=2)),
                                 reads=["ptb"], writes=["KTs"])
                    return consume

                lin_tm(w_in, w_in[l], 16, SEC["aq"], 1024, hT, "hT", 4, qk_consume("q"), wp)
                lin_tm(w_in, w_in[l], 16, SEC["ak"], 256, hT, "hT", 4, qk_consume("k"), wp)

                def v_consume(tb, coff, ncb, ps, pkey):
                    j = nxt()
                    P.op("act", lambda e, j=j: e.copy(out=ev2[j][:], in_=ps), reads=[pkey], writes=["evb%d" % j])
                    P.op("pool", lambda e, j=j: e.tensor_copy(out=obf[j][:, 0:256], in_=ev2[j][:]), reads=["evb%d" % j], writes=["obf%d" % j])
                    dstv = (o_v[l] if isP else JV)
                    P.dma("act", dstv[t0 + tb * 128:t0 + (tb + 1) * 128, :], ev2[j][:], reads=["evb%d" % j], writes=["o_v"])
                    P.dma("act", VV[nctx + t0 + tb * 128:nctx + t0 + (tb + 1) * 128, :], obf[j][:, 0:256], reads=["obf%d" % j], writes=["VV"])
                lin_tm(w_in, w_in[l], 16, SEC["av"], 256, hT, "hT", 4, v_consume, wp)
                P.dma("act", QT[t0 // 128:t0 // 128 + 4].rearrange("tb p n -> p tb n"), QTs[:], reads=["QTs"], writes=["QT"])
                P.dma("act", KT[:, :, nctx + t0:nctx + t0 + 512], KTs[:], reads=["KTs"], writes=["KT"])

                def mqk_consume(dst, dkey, scale):
                    def consume(mi, ps, pkey):
                        h, dkc = mi // 2, mi % 2
                        def f(e):
                            return e.mul(dst[:].rearrange("p tb (h c t) -> p tb h c t", h=4, c=2)[:, :, h, dkc, :],
                                         ps.rearrange("p (tb t) -> p tb t", tb=4), scale)
                        P.op("act", f, reads=[pkey], writes=[dkey])
                    return consume
                lin_fm(w_in, w_in[l], 16, SEC["mq"], 1024, hT, "hT", 512, mqk_consume(MQs, "MQs", 1.0), wp, bank0=0)
                lin_fm(w_in, w_in[l], 16, SEC["mk"], 1024, hT, "hT", 512, mqk_consume(MKs, "MKs", 1.0 / 16.0), wp, bank0=0)
                P.dma("act", MQT[t0 // 128:t0 // 128 + 4].rearrange("tb p n -> p tb n"), MQs[:], reads=["MQs"], writes=["MQT"])
                P.dma("act", MKT[t0 // 128:t0 // 128 + 4].rearrange("tb p n -> p tb n"), MKs[:], reads=["MKs"], writes=["MKT"])

                def tm_consume(dstD, dkey, func, scale):
                    def consume(tb, coff, ncb, ps, pkey):
                        j = evc["i"] % 3
                        evc["i"] += 1
                        P.op("act", lambda e, j=j: e.activation(out=obf[j][:, 0:ncb], in_=ps, func=func, scale=scale),
                             reads=[pkey], writes=["obf%d" % j])
                        P.dma("act", dstD[t0 + tb * 128:t0 + (tb + 1) * 128, coff:coff + ncb], obf[j][:, 0:ncb], reads=["obf%d" % j], writes=[dkey])
                    return consume
                lin_tm(w_in, w_in[l], 16, SEC["mk"], 1024, hT, "hT", 4, tm_consume(MK, "MK", AF.Identity, 1.0 / 16.0), wp)
                lin_tm(w_in, w_in[l], 16, SEC["mv"], 1024, hT, "hT", 4, tm_consume(MV, "MV", AF.Identity, 1.0), wp)
                lin_tm(w_in, w_in[l], 16, SEC["mo"], 1024, hT, "hT", 4, tm_consume(MOS, "MOS", AF.Sigmoid, 1.0), wp)

                def g_consume(tb, coff, ncb, ps, pkey):
                    j = nxt()
                    P.op("dve", lambda e, j=j: e.tensor_tensor(out=og[j][:], in0=ps, in1=gb[:], op=ALU.add), reads=[pkey, "gb"], writes=["og%d" % j])
                    P.dma("act", MG[t0 + tb * 128:t0 + (tb + 1) * 128, :], og[j][:], reads=["og%d" % j], writes=["MG"])
                lin_tm(w_in, w_in[l], 16, SEC["mg"], 16, hT, "hT", 4, g_consume, wp)

                cct = P_cct = None
                def fm_store(dstD, dkey, ch0, func):
                    def consume(mi, ps, pkey):
                        j = evc["i"] % 3
                        evc["i"] += 1
                        P.op("act", lambda e, j=j: e.activation(out=obf[j][:], in_=ps, func=func), reads=[pkey], writes=["obf%d" % j])
                        P.dma("act", dstD[:, ch0 + mi, t0:t0 + 512], obf[j][:], reads=["obf%d" % j], writes=[dkey])
                    return consume
                lin_fm(w_in, w_in[l], 16, SEC["cb"], 1024, hT, "hT", 512, fm_store(CB, "CB", 0, AF.Identity), wp, bank0=0)
                for m2 in range(4):
                    ccs = ccs_t[m2 % 2]
                    def cc_consume(mi, ps, pkey, ccs=ccs, m2=m2):
                        P.op("act", lambda e, mi=mi: e.copy(out=ccs[:, mi, :], in_=ps), reads=[pkey], writes=["ccs%d_%d" % (m2 % 2, mi)])
                    def cx_consume(mi, ps, pkey, ccs=ccs, m2=m2):
                        j = evc["i"] % 3
                        evc["i"] += 1
                        P.op("dve", lambda e, j=j, mi=mi: e.tensor_tensor(out=obf[j][:], in0=ps, in1=ccs[:, mi, :], op=ALU.mult),
                             reads=[pkey, "ccs%d_%d" % (m2 % 2, mi)], writes=["obf%d" % j])
                        P.dma("act", CU[:, m2 * 2 + mi, t0:t0 + 512], obf[j][:], reads=["obf%d" % j], writes=["CU"])
                    lin_fm(w_in, w_in[l], 16, SEC["cc"] + m2 * 256, 256, hT, "hT", 512, cc_consume, wp, bank0=0)
                    lin_fm(w_in, w_in[l], 16, SEC["cx"] + m2 * 256, 256, hT, "hT", 512, cx_consume, wp, bank0=2)
                lin_fm(w_in, w_in[l], 16, SEC["gl"], 6144, hT, "hT", 512, fm_store(GS, "GS", 0, AF.Sigmoid), wp, bank0=0)
            P.end()
            if stop == "A" and l == stop_l:
                P.close()
                return nc

            P.begin()
            KTa = P.sb("KTa", [128, 2, NK], BF16)
            Va = P.sb("Va", [128, NK // 128, 256], BF16)
            P.dma("sp", KTa[:], KT[:, :, 0:NK], reads=["KT"], writes=["KTa"])
            P.dma("sp", Va[:], VV[0:NK, :].rearrange("(c p) n -> p c n", p=128), reads=["VV"], writes=["Va"])
            qb_ = [P.sb("qb%d" % i, [128, 1024], BF16) for i in range(2)]
            pt_ = [P.sb("pt%d" % i, [128, 512], BF16) for i in range(3)]
            rden = [P.sb("rden%d" % i, [128, 512], F32) for i in range(2)]
            ao = [P.sb("ao%d" % i, [128, 512], BF16) for i in range(2)]
            cnt = 0
            for (s0, sl) in job["seqs"]:
                kch = list(range(nctx // 128)) + [(nctx + s0) // 128 + i for i in range(sl // 128)]
                for qi in range(sl // 128):
                    qblk = s0 // 128 + qi
                    j = qblk % 2
                    P.dma("sp", qb_[j][:], QT[qblk], reads=["QT"], writes=["qb%d" % j])
                    for kvh in range(2):
                        bo, bd = (4, 5) if (cnt % 2 == 0) else (2, 3)
                        cnt += 1
                        for ii, kc in enumerate(kch):
                            bs = ii % 2
                            pj = ii % 3
                            P.op("pe", lambda e, j=j, kc=kc, kvh=kvh, bs=bs: e.matmul(pb[bs][:, :], lhsT=KTa[:, kvh, kc * 128:(kc + 1) * 128],
                                                                                      rhs=qb_[j][:, kvh * 512:(kvh + 1) * 512], start=True, stop=True),
                                 reads=["KTa", "qb%d" % j], writes=[PBK[bs]])
                            P.op("act", lambda e, bs=bs, pj=pj: e.activation(out=pt_[pj][:], in_=pb[bs][:, :], func=AF.Exp),
                                 reads=[PBK[bs]], writes=["pt%d" % pj])
                            def fpv(e, kc=kc, kvh=kvh, pj=pj, ii=ii, bo=bo, bd=bd, last=(ii == len(kch) - 1)):
                                e.matmul(pb[bo][:, :], lhsT=Va[:, kc, kvh * 128:(kvh + 1) * 128], rhs=pt_[pj][:], start=(ii == 0), stop=last)
                                return e.matmul(pb[bd][:, :], lhsT=onesb[:], rhs=pt_[pj][:], start=(ii == 0), stop=last)
                            P.op("pe", fpv, reads=["Va", "pt%d" % pj, "onesb"], writes=[PBK[bo], PBK[bd]])
                        jj = cnt % 2
                        P.op("dve", lambda e, jj=jj, bd=bd: e.reciprocal(out=rden[jj][:], in_=pb[bd][:, :]), reads=[PBK[bd]], writes=["rden%d" % jj])
                        P.op("dve", lambda e, jj=jj, bo=bo: e.tensor_tensor(out=ao[jj][:], in0=pb[bo][:, :], in1=rden[jj][:], op=ALU.mult),
                             reads=[PBK[bo], "rden%d" % jj], writes=["ao%d" % jj])
                        P.dma("act", ATT[:, kvh * 4:(kvh + 1) * 4, qblk * 128:(qblk + 1) * 128], ao[jj][:].rearrange("p (h t) -> p h t", h=4),
                              reads=["ao%d" % jj], writes=["ATT"])
            P.end()
            if stop == "B1" and l == stop_l:
                P.close()
                return nc

            for si, (s0, sl) in enumerate(job["seqs"]):
                P.begin()
                nch = sl // 128
                hacc = P.sb("hacc", [128, nch, 1024], F32)
                Cst = [[P.sb("C%d%d" % (d, h), [128, 2, 256], F32) for h in range(4)] for d in range(2)]
                Cb = [[P.sb("Cb%d%d" % (d, h), [128, 2, 257], BF16) for h in range(4)] for d in range(2)]
                nst = [P.sb("n%d" % d, [128, 4, 2], F32) for d in range(2)]
                mbc = P.sb("mbc", [128, 8], F32)
                gm = P.sb("gm", [128, 1024], F32)
                load_row_bc(gm[:], "gm", m_norm[l])
                for d in range(2):
                    for h in range(4):
                        if isP:
                            P.op("pool", lambda e, d=d, h=h: e.memset(Cst[d][h][:], 0.0), writes=["C%d%d" % (d, h)])
                        else:
                            P.dma("sp", Cst[d][h][:], state_C[l, d, h].rearrange("(c p) v -> p c v", p=128), writes=["C%d%d" % (d, h)])
                    if isP:
                        P.op("pool", lambda e, d=d: e.memset(nst[d][:], 0.0), writes=["n%d" % d])
                    else:
                        P.dma("sp", nst[d][:], state_n[l, d].rearrange("h (c p) -> p h c", p=128), writes=["n%d" % d], allow_slow_non_contiguous=True)
                    for h in range(4):
                        def fcb(e, d=d, h=h):
                            e.copy(out=Cb[d][h][:, :, 0:256], in_=Cst[d][h][:])
                            return e.copy(out=Cb[d][h][:, :, 256:257], in_=nst[d][:, h, :].unsqueeze(2))
                        P.op("act", fcb, reads=["C%d%d" % (d, h), "n%d" % d], writes=["Cb%d%d" % (d, h)])
                if isP:
                    P.op("pool", lambda e: e.memset(mbc[:], 0.0), writes=["mbc"])
                else:
                    load_row_bc(mbc[:], "mbc", state_m[l], stream="sp")

                NB = 2
                qTc = [P.sb("qTc%d" % i, [128, 4, 2, 128], BF16) for i in range(NB)]
                kTc = [P.sb("kTc%d" % i, [128, 4, 2, 128], BF16) for i in range(NB)]
                kc_ = [P.sb("kc%d" % i, [128, 1024], BF16) for i in range(NB)]
                va = [P.sb("va%d" % i, [128, 4, 257], BF16) for i in range(NB)]
                mg_ = [P.sb("mg%d" % i, [128, 16], F32) for i in range(NB)]
                for i in range(NB):
                    P.op("pool", lambda e, i=i: e.memset(va[i][:, :, 256:257], 1.0), writes=["va1_%d" % i])
                sp_ = P.sb("sp_", [128, 4], F32)
                ex_ = P.sb("ex_", [128, 4], F32)
                negb = P.sb("negb", [128, 8], F32)
                a_ = P.sb("a_", [128, 4], F32)
                diag = P.sb("diag", [128, 4, 128], F32)
                Am = P.sb("Am", [128, 4, 128], F32)
                Dm = P.sb("Dm", [128, 4, 128], F32)
                cmx = P.sb("cmx", [128, 4], F32)
                Mx = P.sb("Mx", [128, 4], F32)
                nM = P.sb("nM", [128, 4], F32)
                ML = P.sb("ML", [128, 4], F32)
                itr = P.sb("itr", [128, 4], F32)
                clp = P.sb("clp", [128, 4], F32)
                wv_ = P.sb("wv_", [128, 4], F32)
                dec = P.sb("dec", [128, 4], F32)
                t4 = P.sb("t4", [128, 4], F32)
                Sp = P.sb("Sp", [128, 4, 128], BF16)
                SpT = P.sb("SpT", [128, 4, 128], BF16)
                tI = [P.sb("tI%d" % i, [128, 257], F32) for i in range(2)]
                tot = [P.sb("tot%d" % i, [128, 257], F32) for i in range(2)]
                dn = [P.sb("dn%d" % i, [128, 2], F32) for i in range(2)]
                wvv = P.sb("wvv", [128, 4, 257], BF16)
                step = {"i": 0}

                def chunk_step(d, c):
                    i = step["i"] % NB
                    step["i"] += 1
                    ch = s0 // 128 + c
                    tok0 = s0 + c * 128
                    j0 = d * 4
                    P.dma("sp", qTc[i][:], MQT[ch].rearrange("p (h c t) -> p h c t", h=4, c=2), reads=["MQT"], writes=["qTc%d" % i])
                    P.dma("sp", kTc[i][:], MKT[ch].rearrange("p (h c t) -> p h c t", h=4, c=2), reads=["MKT"], writes=["kTc%d" % i])
                    P.dma("sp", kc_[i][:], MK[tok0:tok0 + 128, :], reads=["MK"], writes=["kc%d" % i])
                    P.dma("sp", va[i][:, :, 0:256], MV[tok0:tok0 + 128, :].rearrange("p (h v) -> p h v", h=4), reads=["MV"], writes=["va%d" % i])
                    P.dma("sp", mg_[i][:], MG[tok0:tok0 + 128, :], reads=["MG"], writes=["mg%d" % i])
                    ipre = mg_[i][:, d * 8:d * 8 + 4]
                    fpre = mg_[i][:, d * 8 + 4:d * 8 + 8]
                    tri = triF if d == 0 else triB
                    msk = maskF if d == 0 else maskB
                    def fsp(e):
                        e.activation(out=ex_[:], in_=fpre, func=AF.Exp, scale=-1.0)
                        return e.activation(out=sp_[:], in_=ex_[:], func=AF.Ln, bias=1.0)
                    P.op("act", fsp, reads=["mg%d" % i], writes=["ex_", "sp_"])
                    def fcs(e):
                        e.matmul(pb[6][:, 0:4], lhsT=tri, rhs=sp_[:], start=True, stop=True)
                        return e.matmul(pb[6][:, 4:8], lhsT=onesf[:], rhs=sp_[:], start=True, stop=True)
                    P.op("pe", fcs, reads=["sp_", "cm", "onesf"], writes=[PBK[6]])
                    def fa(e):
                        e.tensor_copy(out=negb[:], in_=pb[6][:, 0:8])
                        e.tensor_tensor(out=a_[:], in0=ipre, in1=negb[:, 0:4], op=ALU.add)
                        ins = None
                        for h in range(4):
                            ins = e.tensor_scalar(out=diag[:, h, :], in0=identf[:], scalar1=a_[:, h:h + 1], scalar2=None, op0=ALU.mult)
                        return ins
                    P.op("dve", fa, reads=[PBK[6], "mg%d" % i, "identf"], writes=["negb", "a_", "diag"])
                    def fbc(e):
                        ins = None
                        for h in range(4):
                            ins = e.matmul(pb[5][:, h * 128:(h + 1) * 128], lhsT=onesf[:], rhs=diag[:, h, :], start=True, stop=True)
                        return ins
                    P.op("pe", fbc, reads=["diag", "onesf"], writes=[PBK[5]])
                    def fm(e):
                        abc = pb[5][:, :].rearrange("p (h s) -> p h s", h=4)
                        e.tensor_reduce(out=ML[:], in_=abc, axis=AX.X, op=ALU.max)
                        e.tensor_tensor(out=Am[:], in0=abc, in1=msk.unsqueeze(1).to_broadcast([128, 4, 128]), op=ALU.add)
                        e.tensor_reduce(out=cmx[:], in_=Am[:], axis=AX.X, op=ALU.max)
                        e.tensor_tensor(out=Mx[:], in0=cmx[:], in1=mbc[:, j0:j0 + 4], op=ALU.max)
                        e.tensor_scalar(out=nM[:], in0=Mx[:], scalar1=-1.0, scalar2=None, op0=ALU.mult)
                        e.tensor_tensor(out=ML[:], in0=ML[:], in1=mbc[:, j0:j0 + 4], op=ALU.max)
                        e.tensor_tensor(out=itr[:], in0=mbc[:, j0:j0 + 4], in1=Mx[:], op=ALU.subtract)
                        e.tensor_tensor(out=clp[:], in0=negb[:, 0:4], in1=Mx[:], op=ALU.subtract)
                        e.tensor_tensor(out=wv_[:], in0=a_[:], in1=ML[:], op=ALU.subtract)
                        e.tensor_tensor(out=dec[:], in0=mbc[:, j0:j0 + 4], in1=ML[:], op=ALU.subtract)
                        return e.tensor_tensor(out=mbc[:, j0:j0 + 4], in0=ML[:], in1=negb[:, 4:8], op=ALU.subtract)
                    P.op("dve", fm, reads=[PBK[5], "cm", "mbc", "negb", "a_"], writes=["Am", "cmx", "Mx", "nM", "ML", "itr", "clp", "wv_", "dec", "mbc"])
                    def fex(e):
                        for h in range(4):
                            e.activation(out=Dm[:, h, :], in_=Am[:, h, :], func=AF.Exp, bias=nM[:, h:h + 1])
                        e.activation(out=itr[:], in_=itr[:], func=AF.Exp)
                        e.activation(out=clp[:], in_=clp[:], func=AF.Exp)
                        e.activation(out=wv_[:], in_=wv_[:], func=AF.Exp)
                        return e.activation(out=dec[:], in_=dec[:], func=AF.Exp)
                    P.op("act", fex, reads=["Am", "nM", "itr", "clp", "wv_", "dec"], writes=["Dm", "itr", "clp", "wv_", "dec"])
                    def fS(e):
                        ins = None
                        for h in range(4):
                            for c2 in range(2):
                                ins = e.matmul(pb[4][:, h * 128:(h + 1) * 128], lhsT=qTc[i][:, h, c2, :], rhs=kTc[i][:, h, c2, :],
                                               start=(c2 == 0), stop=(c2 == 1))
                        return ins
                    P.op("pe", fS, reads=["qTc%d" % i, "kTc%d" % i], writes=[PBK[4]])
                    P.op("dve", lambda e: e.tensor_tensor(out=Sp[:], in0=pb[4][:, :].rearrange("p (h s) -> p h s", h=4), in1=Dm[:], op=ALU.mult),
                         reads=[PBK[4], "Dm"], writes=["Sp"])
                    def fT(e):
                        ins = None
                        for h in range(4):
                            ins = e.transpose(ptb[:, h * 128:(h + 1) * 128], Sp[:, h, :], ident[:])
                        return ins
                    P.op("pe", fT, reads=["Sp", "ident"], writes=["ptb"])
                    P.op("act", lambda e: e.copy(out=SpT[:].rearrange("p h t -> p (h t)"), in_=ptb[:, 0:512]), reads=["ptb"], writes=["SpT"])
                    first = (c < nch // 2) == (d == 0)
                    for h in range(4):
                        jj = h % 2
                        bI, bL = (0, 1) if jj == 0 else (2, 3)
                        def fI(e, h=h, bI=bI, bL=bL):
                            for c2 in range(2):
                                e.matmul(pb[bI][:, 0:257], lhsT=qTc[i][:, h, c2, :], rhs=Cb[d][h][:, c2, :], start=(c2 == 0), stop=(c2 == 1))
                            return e.matmul(pb[bL][:, 0:257], lhsT=SpT[:, h, :], rhs=va[i][:, h, :], start=True, stop=True)
                        P.op("pe", fI, reads=["qTc%d" % i, "Cb%d%d" % (d, h), "SpT", "va%d" % i, "va1_%d" % i], writes=[PBK[bI], PBK[bL]])
                        P.op("act", lambda e, h=h, jj=jj, bI=bI: e.activation(out=tI[jj][:], in_=pb[bI][:, 0:257], func=AF.Identity, scale=itr[:, h:h + 1]),
                             reads=[PBK[bI], "itr"], writes=["tI%d" % jj])
                        def fh(e, h=h, jj=jj, bL=bL):
                            e.tensor_tensor(out=tot[jj][:], in0=tI[jj][:], in1=pb[bL][:, 0:257], op=ALU.add)
                            e.tensor_scalar(out=dn[jj][:, 0:1], in0=tot[jj][:, 256:257], scalar1=-1.0, scalar2=None, op0=ALU.mult)
                            e.tensor_tensor(out=dn[jj][:, 0:1], in0=dn[jj][:, 0:1], in1=tot[jj][:, 256:257], op=ALU.max)
                            e.tensor_tensor(out=dn[jj][:, 0:1], in0=dn[jj][:, 0:1], in1=clp[:, h:h + 1], op=ALU.max)
                            e.reciprocal(out=dn[jj][:, 1:2], in_=dn[jj][:, 0:1])
                            dst = hacc[:, c, h * 256:(h + 1) * 256]
                            if first:
                                return e.tensor_scalar(out=dst, in0=tot[jj][:, 0:256], scalar1=dn[jj][:, 1:2], scalar2=None, op0=ALU.mult)
                            return e.scalar_tensor_tensor(out=dst, in0=tot[jj][:, 0:256], scalar=dn[jj][:, 1:2], in1=dst, op0=ALU.mult, op1=ALU.add)
                        P.op("dve", fh, reads=["tI%d" % jj, PBK[bL], "clp"], writes=["tot%d" % jj, "dn%d" % jj, "hacc%d" % c])
                    P.op("dve", lambda e: e.tensor_tensor(out=wvv[:], in0=va[i][:], in1=wv_[:].unsqueeze(2).to_broadcast([128, 4, 257]), op=ALU.mult),
                         reads=["va%d" % i, "va1_%d" % i, "wv_"], writes=["wvv"])
                    for h in range(4):
                        for c2 in range(2):
                            bU = (h * 2 + c2) % 4
                            P.op("pe", lambda e, h=h, c2=c2, bU=bU: e.matmul(pb[bU][:, 0:257], lhsT=kc_[i][:, h * 256 + c2 * 128:h * 256 + (c2 + 1) * 128],
                                                                                rhs=wvv[:, h, :], start=True, stop=True),
                                 reads=["kc%d" % i, "wvv"], writes=[PBK[bU]])
                            def fu(e, h=h, c2=c2, bU=bU):
                                e.scalar_tensor_tensor(out=Cst[d][h][:, c2, :], in0=Cst[d][h][:, c2, :], scalar=dec[:, h:h + 1], in1=pb[bU][:, 0:256],
                                                       op0=ALU.mult, op1=ALU.add)
                                return e.scalar_tensor_tensor(out=nst[d][:, h, c2:c2 + 1], in0=nst[d][:, h, c2:c2 + 1], scalar=dec[:, h:h + 1],
                                                              in1=pb[bU][:, 256:257], op0=ALU.mult, op1=ALU.add)
                            P.op("dve", fu, reads=[PBK[bU], "dec", "C%d%d" % (d, h), "n%d" % d, "Cb%d%d" % (d, h)], writes=["C%d%d" % (d, h), "n%d" % d])
                        def fcb(e, h=h):
                            e.copy(out=Cb[d][h][:, :, 0:256], in_=Cst[d][h][:])
                            return e.copy(out=Cb[d][h][:, :, 256:257], in_=nst[d][:, h, :].unsqueeze(2))
                        P.op("act", fcb, reads=["C%d%d" % (d, h), "n%d" % d], writes=["Cb%d%d" % (d, h)])

                for stp in range(nch):
                    chunk_step(0, stp)
                    chunk_step(1, nch - 1 - stp)

                bsel = si
                for d in range(2):
                    for h in range(4):
                        dC = o_C[bsel, l, d, h] if isP else JC[d, h]
                        P.dma("act", dC.rearrange("(c p) v -> p c v", p=128), Cst[d][h][:], reads=["C%d%d" % (d, h)], writes=["o_C"])
                    dN = o_n[bsel, l, d] if isP else JN[d]
                    P.dma("act", dN.rearrange("h (c p) -> p h c", p=128), nst[d][:], reads=["n%d" % d], writes=["o_n"], allow_slow_non_contiguous=True)
                dM = o_m[bsel, l] if isP else JM
                P.dma("act", dM.unsqueeze(0), mbc[0:1, :], reads=["mbc"], writes=["o_m"])

                mos = [P.sb("mos%d" % i2, [128, 1024], BF16) for i2 in range(2)]
                hsq = P.sb("hsq", [128, 1024], F32)
                hs4 = P.sb("hs4", [128, 4], F32)
                hmb = [P.sb("hmb%d" % i2, [128, 1024], BF16) for i2 in range(2)]
                mlt = [P.sb("mlt%d" % i2, [128, 8, 128], BF16) for i2 in range(2)]
                for c in range(nch):
                    i2 = c % 2
                    tok0 = s0 + c * 128
                    P.dma("sp", mos[i2][:], MOS[tok0:tok0 + 128, :], reads=["MOS"], writes=["mos%d" % i2])
                    def fhn0(e, c=c):
                        hv = hacc[:, c, :]
                        e.tensor_tensor(out=hsq[:], in0=hv, in1=hv, op=ALU.mult)
                        return e.tensor_reduce(out=hs4[:], in_=hsq[:].rearrange("p (h v) -> p h v", h=4), axis=AX.X, op=ALU.add)
                    P.op("dve", fhn0, reads=["hacc%d" % c], writes=["hsq", "hs4"])
                    P.op("act", lambda e: act_rstd(e, hs4[:], hs4[:], 256), reads=["hs4"], writes=["hs4"])
                    def fhn(e, c=c, i2=i2):
                        hv = hacc[:, c, :]
                        e.tensor_tensor(out=hsq[:].rearrange("p (h v) -> p h v", h=4), in0=hv.rearrange("p (h v) -> p h v", h=4),
                                        in1=hs4[:].unsqueeze(2).to_broadcast([128, 4, 256]), op=ALU.mult)
                        e.tensor_tensor(out=hsq[:], in0=hsq[:], in1=gm[:], op=ALU.mult)
                        return e.tensor_tensor(out=hmb[i2][:], in0=hsq[:], in1=mos[i2][:], op=ALU.mult)
                    P.op("dve", fhn, reads=["hacc%d" % c, "gm", "mos%d" % i2, "hs4"], writes=["hsq", "hmb%d" % i2])
                    def fT2(e, i2=i2):
                        ins = None
                        for k in range(8):
                            ins = e.transpose(ptb[:, k * 128:(k + 1) * 128], hmb[i2][:, k * 128:(k + 1) * 128], ident[:])
                        return ins
                    P.op("pe", fT2, reads=["hmb%d" % i2, "ident"], writes=["ptb"])
                    P.op("act", lambda e, i2=i2: e.copy(out=mlt[i2][:].rearrange("p k t -> p (k t)"), in_=ptb[:, :]), reads=["ptb"], writes=["mlt%d" % i2])
                    P.dma("act", MLT[:, :, tok0:tok0 + 128], mlt[i2][:], reads=["mlt%d" % i2], writes=["MLT"])
                P.end()

            if stop == "B2" and l == stop_l:
                P.close()
                return nc
            P.begin()
            wp = WPool()
            mr = [P.sb("mr%d" % i, [128, D], F32) for i in range(2)]
            MODCk = "MODC_%d_%s" % (l, job["name"])
            def comb(idx, mod_off, nrm, add1):
                load_row_bc(mr[0][:], "mr0", MOD[l, ci, mod_off:mod_off + D], extra_reads=["MOD"])
                load_row_bc(mr[1][:], "mr1", nrm)
                if add1:
                    P.op("dve", lambda e: e.scalar_tensor_tensor(out=mr[0][:], in0=mr[0][:], scalar=1.0, in1=mr[1][:], op0=ALU.add, op1=ALU.mult),
                         reads=["mr0", "mr1"], writes=["mr0"])
                else:
                    P.op("dve", lambda e: e.tensor_tensor(out=mr[0][:], in0=mr[0][:], in1=mr[1][:], op=ALU.mult), reads=["mr0", "mr1"], writes=["mr0"])
                P.dma("act", MODC[idx:idx + 1, :], mr[0][0:1, :], reads=["mr0"], writes=[MODCk])
            comb(0, 2 * D, n_post1[l], False)
            comb(1, 4 * D, n_pre2[l], True)
            comb(2, 5 * D, n_post2[l], False)
            mrc = {"i": 0}
            def mrow(idx):
                j = mrc["i"] % 2
                mrc["i"] += 1
                if idx == 3:
                    load_row_bc(mr[j][:], "mr%d" % j, MOD[l, ci, 3 * D:4 * D], extra_reads=["MOD"])
                else:
                    load_row_bc(mr[j][:], "mr%d" % j, MODC[idx], extra_reads=[MODCk])
                return mr[j], "mr%d" % j
            cw = P.sb("cw", [128, 3, 8], F32)
            P.dma("act", cw[:], conv_w[l].rearrange("k (c p) -> p k c", p=128), writes=["cw"], allow_slow_non_contiguous=True)

            hT = P.sb("hT", [128, 16, 512], BF16)
            bins = [P.sb("bin%d" % g, [128, 8, 512], BF16) for g in range(3)]
            cut = [P.sb("cut%d" % i, [128, 514], BF16) for i in range(2)]
            cbt = [P.sb("cbt%d" % i, [128, 512], BF16) for i in range(2)]
            ctmp = P.sb("ctmp", [128, 512], F32)
            mtmp = [P.sb("mtmp%d" % i, [128, 512], F32) for i in range(2)]
            mix = P.sb("mix", [128, 4, D], BF16)
            ssq = P.sb("ssq", [128, 4, 8], F32)
            rs = P.sb("rs", [128, 8], F32)
            xs = [P.sb("xs%d" % i, [128, D], F32) for i in range(2)]
            tmpf = P.sb("tmpf", [128, D], F32)
            htm = [P.sb("htm%d" % i, [128, D], BF16) for i in range(2)]
            junk = P.sb("junk", [128, WC], F32)
            actT = P.sb("actT", [128, 44, 512], BF16)
            gsl = [P.sb("gsl%d" % i, [128, 3, 512], BF16) for i in range(2)]
            sgt = [P.sb("sg%d" % i, [128, 2, 512], F32) for i in range(2)]
            ec = {"i": 0}

            def transposes_to_hT(j, hk, tb):
                for g in range(2):
                    def ftr(e, j=j, g=g):
                        ins = None
                        for kk in range(8):
                            kc = g * 8 + kk
                            ins = e.transpose(ptb[:, kk * 128:(kk + 1) * 128], htm[j][:, kc * 128:(kc + 1) * 128], ident[:])
                        return ins
                    P.op("pe", ftr, reads=[hk, "ident"], writes=["ptb"])
                    P.op("act", lambda e, g=g, tb=tb: e.copy(out=hT[:, g * 8:(g + 1) * 8, tb * 128:(tb + 1) * 128],
                                                             in_=ptb[:].rearrange("p (k t) -> p k t", k=8)),
                         reads=["ptb"], writes=["hT"])

            for ti in range(ntile):
                t0 = ti * 512
                P.dma("sp", bins[0][:], ATT[:, :, t0:t0 + 512], reads=["ATT"], writes=["bin0"])
                P.dma("sp", bins[1][:], MLT[:, :, t0:t0 + 512], reads=["MLT"], writes=["bin1"])
                for ch in range(8):
                    j = ch % 2
                    lo_h = t0 - 1 if t0 - 1 >= 0 else t0
                    hi_h = t0 + 513 if t0 + 513 <= T else t0 + 512
                    P.dma("sp", cut[j][:, 1 + (lo_h - t0):1 + (hi_h - t0)], CU[:, ch, lo_h:hi_h], reads=["CU"], writes=["cut%d" % j])
                    P.dma("sp", cbt[j][:], CB[:, ch, t0:t0 + 512], reads=["CB"], writes=["cbt%d" % j])
                    for (s0, sl) in job["seqs"]:
                        lo, hi = max(s0, t0), min(s0 + sl, t0 + 512)
                        if lo >= hi:
                            continue
                        a, b = lo - t0, hi - t0
                        zl = (lo == s0)
                        zr = (hi == s0 + sl)
                        def fcv(e, ch=ch, j=j, a=a, b=b, zl=zl, zr=zr):
                            e.tensor_scalar(out=ctmp[:, a:b], in0=cut[j][:, 1 + a:1 + b], scalar1=cw[:, 1, ch:ch + 1], scalar2=None, op0=ALU.mult)
                            a1 = a + 1 if zl else a
                            e.scalar_tensor_tensor(out=ctmp[:, a1:b], in0=cut[j][:, a1:b], scalar=cw[:, 0, ch:ch + 1], in1=ctmp[:, a1:b],
                                                   op0=ALU.mult, op1=ALU.add)
                            b1 = b - 1 if zr else b
                            e.scalar_tensor_tensor(out=ctmp[:, a:b1], in0=cut[j][:, 2 + a:2 + b1], scalar=cw[:, 2, ch:ch + 1], in1=ctmp[:, a:b1],
                                                   op0=ALU.mult, op1=ALU.add)
                            return e.tensor_tensor(out=bins[2][:, ch, a:b], in0=ctmp[:, a:b], in1=cbt[j][:, a:b], op=ALU.mult)
                        P.op("dve", fcv, reads=["cut%d" % j, "cbt%d" % j, "cw"], writes=["ctmp", "bin2"])
                if stop == "C0" and l == stop_l:
                    P.end()
                    P.close()
                    return nc
                for bi in range(D // WC):
                    for g in range(3):
                        bf, kb = wp.block(w_branch[l, g], 0, 8, bi * WC, WC)
                        for m in range(2):
                            bk = g * 2 + m
                            def f(e, bf=bf, g=g, m=m, bk=bk):
                                ins = None
                                for kc in range(8):
                                    ins = e.matmul(pb[bk][:, :], lhsT=bf[:, kc, m * 128:(m + 1) * 128], rhs=bins[g][:, kc, :],
                                                   start=(kc == 0), stop=(kc == 7))
                                return ins
                            P.op("pe", f, reads=[kb, "bin%d" % g], writes=PBK[bk])
                    for m in range(2):
                        mi = bi * 2 + m
                        j = ec["i"] % 2
                        ec["i"] += 1
                        P.dma("act", gsl[j][:], GS[:, :, t0:t0 + 512].rearrange("p (g c) t -> p g c t", g=3)[:, :, mi, :], reads=["GS"], writes=["gsl%d" % j])
                        def fmg(e, j=j, m=m, mi=mi):
                            e.tensor_tensor(out=mtmp[0][:], in0=pb[0 + m][:, :], in1=gsl[j][:, 0, :], op=ALU.mult)
                            e.tensor_tensor(out=mtmp[1][:], in0=pb[2 + m][:, :], in1=gsl[j][:, 1, :], op=ALU.mult)
                            e.tensor_tensor(out=mtmp[0][:], in0=mtmp[0][:], in1=mtmp[1][:], op=ALU.add)
                            e.tensor_tensor(out=mtmp[1][:], in0=pb[4 + m][:, :], in1=gsl[j][:, 2, :], op=ALU.mult)
                            return e.tensor_tensor(out=hT[:, mi, :], in0=mtmp[0][:], in1=mtmp[1][:], op=ALU.add)
                        P.op("dve", fmg, reads=[PBK[m], PBK[2 + m], PBK[4 + m], "gsl%d" % j], writes=["mtmp0", "mtmp1", "hT"])

                if stop == "C1" and l == stop_l:
                    P.end()
                    P.close()
                    return nc
                def mix_consume(tb, coff, ncb, ps, pkey):
                    bi = coff // WC
                    P.op("act", lambda e, tb=tb: e.copy(out=mix[:, tb, coff:coff + ncb], in_=ps), reads=[pkey], writes=["mix"])
                lin_tm(w_out, w_out[l], 16, 0, D, hT, "hT", 4, mix_consume, wp)
                if stop == "C2" and l == stop_l:
                    P.end()
                    P.close()
                    return nc
                G1t, G1k = mrow(0)
                for tb in range(4):
                    j = tb % 2
                    xk = "xs%d" % j
                    P.dma("sp", xs[j][:], x_src[t0 + tb * 128:t0 + (tb + 1) * 128, :], writes=[xk])
                    def fms(e, tb=tb):
                        e.memzero(rs[:, 1:2])
                        e.activation(out=tmpf[:], in_=mix[:, tb, :], func=AF.Square, accum_out=rs[:, 1:2])
                        return act_rstd(e, rs[:, 0:1], rs[:, 1:2], D)
                    P.op("act", fms, reads=["mix"], writes=["tmpf", "rs"])
                    def fx(e, j=j, tb=tb, G1t=G1t):
                        e.scalar_tensor_tensor(out=tmpf[:], in0=mix[:, tb, :], scalar=rs[:, 0:1], in1=G1t[:], op0=ALU.mult, op1=ALU.mult)
                        return e.tensor_tensor(out=xs[j][:], in0=xs[j][:], in1=tmpf[:], op=ALU.add)
                    P.op("dve", fx, reads=[xk, "rs", "mix", G1k], writes=["tmpf", xk])
                    P.dma("act", XMID[t0 + tb * 128:t0 + (tb + 1) * 128, :], xs[j][:], reads=[xk], writes=["XMID"])
                A2t, A2k = mrow(1)
                B2t, B2k = mrow(3)
                for tb in range(4):
                    j = tb % 2
                    xk, hk = "xs%d" % j, "htm%d" % j
                    P.dma("sp", xs[j][:], XMID[t0 + tb * 128:t0 + (tb + 1) * 128, :], reads=["XMID"], writes=[xk])
                    def fsq(e, j=j):
                        e.memzero(rs[:, 2:3])
                        e.activation(out=htm[j][:], in_=xs[j][:], func=AF.Square, accum_out=rs[:, 2:3])
                        return act_rstd(e, rs[:, 3:4], rs[:, 2:3], D)
                    P.op("act", fsq, reads=[xk], writes=[hk, "rs2"])
                    def fn2(e, j=j, A2t=A2t, B2t=B2t):
                        e.scalar_tensor_tensor(out=tmpf[:], in0=xs[j][:], scalar=rs[:, 3:4], in1=A2t[:], op0=ALU.mult, op1=ALU.mult)
                        return e.tensor_tensor(out=htm[j][:], in0=tmpf[:], in1=B2t[:], op=ALU.add)
                    P.op("dve", fn2, reads=[xk, "rs2", A2k, B2k], writes=["tmpf", hk])
                    transposes_to_hT(j, hk, tb)
                if stop == "C3" and l == stop_l:
                    P.end()
                    P.close()
                    return nc
                for mc in range(FF // 256):
                    sg = sgt[mc % 2]
                    sk = "sg%d_" % (mc % 2)
                    def gate_consume(mi, ps, pkey, sg=sg, sk=sk):
                        P.op("act", lambda e, mi=mi: e.activation(out=sg[:, mi, :], in_=ps, func=AF.Silu), reads=[pkey], writes=[sk + str(mi)])
                    def up_consume(mi, ps, pkey, sg=sg, mc=mc, sk=sk):
                        P.op("dve", lambda e, mi=mi: e.tensor_tensor(out=actT[:, mc * 2 + mi, :], in0=ps, in1=sg[:, mi, :], op=ALU.mult),
                             reads=[pkey, sk + str(mi)], writes=["actT"])
                    lin_fm(w_ffn_in, w_ffn_in[l], 16, mc * 256, 256, hT, "hT", 512, gate_consume, wp, bank0=0)
                    lin_fm(w_ffn_in, w_ffn_in[l], 16, FF + mc * 256, 256, hT, "hT", 512, up_consume, wp, bank0=2)
                if stop == "C4" and l == stop_l:
                    P.end()
                    P.close()
                    return nc
                lin_tm(w_ffn_out, w_ffn_out[l], 44, 0, D, actT, "actT", 4, mix_consume, wp)
                G2t, G2k = mrow(2)
                for tb in range(4):
                    j = tb % 2
                    xk = "xs%d" % j
                    P.dma("sp", xs[j][:], XMID[t0 + tb * 128:t0 + (tb + 1) * 128, :], reads=["XMID"], writes=[xk])
                    def fms(e, tb=tb):
                        e.memzero(rs[:, 1:2])
                        e.activation(out=tmpf[:], in_=mix[:, tb, :], func=AF.Square, accum_out=rs[:, 1:2])
                        return act_rstd(e, rs[:, 0:1], rs[:, 1:2], D)
                    P.op("act", fms, reads=["mix"], writes=["tmpf", "rs"])
                    def fx2(e, j=j, tb=tb, G2t=G2t):
                        e.scalar_tensor_tensor(out=tmpf[:], in0=mix[:, tb, :], scalar=rs[:, 0:1], in1=G2t[:], op0=ALU.mult, op1=ALU.mult)
                        return e.tensor_tensor(out=xs[j][:], in0=xs[j][:], in1=tmpf[:], op=ALU.add)
                    P.op("dve", fx2, reads=[xk, "rs", "mix", G2k], writes=["tmpf", xk])
                    P.dma("act", x_dst[t0 + tb * 128:t0 + (tb + 1) * 128, :], xs[j][:], reads=[xk], writes=["XOUT"])
            P.end()

    P.close()
    return nc


_CACHE = {}


def _consts():
    rows = np.repeat(np.arange(TS // 64), 64)
    cols = np.tile(np.arange(64), TS // 64)
    inv = (10000.0 ** (-np.arange(32, dtype=np.float32) / 32)).astype(np.float32)
    ang = np.stack([rows, cols], axis=-1).astype(np.float32)[:, :, None] * inv
    cosS = np.cos(ang).reshape(TS, 64).astype(np.float32)
    sinS = np.sin(ang).reshape(TS, 64).astype(np.float32)
    cosT = np.concatenate([cosS, np.ones((TP, 64), np.float32)], 0)
    sinT = np.concatenate([sinS, np.zeros((TP, 64), np.float32)], 0)
    t = np.arange(128)
    le = (t[None, :] <= t[:, None])
    maskF = np.where(le, 0.0, NEG).astype(np.float32)
    maskB = np.where(le.T, 0.0, NEG).astype(np.float32)
    triF = (t[:, None] <= t[None, :]).astype(np.float32)
    triB = (t[:, None] >= t[None, :]).astype(np.float32)
    cmask = np.stack([maskF, maskB, triF, triB], 0)
    return cosT, sinT, cmask


def kernel(**inp):
    f = lambda a: np.ascontiguousarray(np.asarray(a, dtype=np.float32))
    if "nc" not in _CACHE:
        _CACHE["nc"] = build_program()
    nc = _CACHE["nc"]
    cosT, sinT, cmask = _consts()
    shared = {k: f(inp[k]) for k in ("w_mod", "b_mod", "norm_pre1", "norm_post1", "norm_pre2", "norm_post2", "w_in",
                                      "q_norm", "k_norm", "mlstm_gate_bias", "mlstm_norm", "conv_w", "w_branch", "w_out",
                                      "w_ffn_in", "w_ffn_out")}
    shared.update(cosT=cosT, sinT=sinT, cmask=cmask)
    x_prompt, x_sample = f(inp["x_prompt"]), f(inp["x_sample"])
    in_maps = []
    for c in range(8):
        b = c % 2
        m = dict(shared)
        m["x_s"] = x_sample[b]
        m["x_p"] = x_prompt[2 * c:2 * c + 2].reshape(TP, D)
        m["cache_k"] = f(inp["cache_k"])[b].reshape(DEPTH, NCTX, 256)
        m["cache_v"] = f(inp["cache_v"])[b].reshape(DEPTH, NCTX, 256)
        m["state_C"] = f(inp["state_C"])[b]
        m["state_n"] = f(inp["state_n"])[b]
        m["state_m"] = f(inp["state_m"])[b].reshape(DEPTH, 8)
        m["cond"] = np.stack([f(inp["c_ctx"]), f(inp["c"])[b]], 0)
        in_maps.append(m)
    res = run_bass_kernel_spmd(nc, in_maps, core_ids=list(range(8)))
    R = res.results
    y_prompt = np.concatenate([R[c]["y_p"].reshape(2, 256, D) for c in range(8)], 0)
    y_sample = np.stack([R[0]["y_s"], R[1]["y_s"]], 0)
    nk = np.concatenate([R[c]["o_k"].reshape(DEPTH, 2, 256, 2, 128).transpose(1, 0, 2, 3, 4) for c in range(8)], 0)
    nv = np.concatenate([R[c]["o_v"].reshape(DEPTH, 2, 256, 2, 128).transpose(1, 0, 2, 3, 4) for c in range(8)], 0)
    nC = np.concatenate([R[c]["o_C"] for c in range(8)], 0)
    nn = np.concatenate([R[c]["o_n"] for c in range(8)], 0)
    nm = np.concatenate([R[c]["o_m"].reshape(2, DEPTH, 2, 4) for c in range(8)], 0)
    return (y_prompt.astype(np.float32), y_sample.astype(np.float32), np.ascontiguousarray(nk), np.ascontiguousarray(nv),
            nC, nn, nm)
```

```python
import contextlib
import numpy as np
import concourse.bass as bass
import concourse.mybir as mybir
from concourse.bass_utils import run_bass_kernel_spmd

F32 = mybir.dt.float32
BF16 = mybir.dt.bfloat16
ALU = mybir.AluOpType
AF = mybir.ActivationFunctionType
AX = mybir.AxisListType

STREAMS = ("sp", "act", "dve", "pool", "pe")

D = 2048
DEPTH = 2
TS = 2048
TP = 512
NCTX = 256
FF = 5632
INW = 14864
SEC = dict(aq=0, ak=1024, av=1280, mq=1536, mk=2560, mv=3584, mo=4608, mg=5632,
           cb=5648, cc=6672, cx=7696, gl=8720)
EPS = 1e-6
WC = 256
WK = 8
NEG = -1.0e30


class Prog:
    def __init__(self, nc):
        self.nc = nc
        self.gstack = contextlib.ExitStack()
        self.NSLOT = 16
        self.dma_i = {"sp": 0, "act": 0, "pool": 0}
        self.sem_names = ["c_act", "c_dve", "c_pool", "c_pe"] + ["d_sp%d" % k for k in range(self.NSLOT)]
        self.sems = {n: self.gstack.enter_context(nc.semaphore(n)) for n in self.sem_names}
        self.cnt = {n: 0 for n in self.sem_names}
        self.seen = {s: {} for s in STREAMS}
        self.lastw = {}
        self.readers = {}
        self.ops = {s: [] for s in STREAMS}
        self.pstack = None
        self.n_ops = 0
        self.emitted = {n: 0 for n in self.sem_names}
        self.dbg = None

    def gsb(self, name, shape, dt):
        return self.gstack.enter_context(self.nc.sbuf_tensor(name, list(shape), dt))

    def gps(self, name, shape, dt):
        return self.gstack.enter_context(self.nc.psum_tensor(name, list(shape), dt))

    def begin(self):
        self.pstack = contextlib.ExitStack()

    def sb(self, name, shape, dt):
        self.uid = getattr(self, "uid", 0) + 1
        return self.pstack.enter_context(self.nc.sbuf_tensor("%s_u%d" % (name, self.uid), list(shape), dt))

    @staticmethod
    def _flat(keys):
        out = []
        for k in keys:
            if isinstance(k, (list, tuple)):
                out.extend(Prog._flat(k))
            else:
                out.append(k)
        return out

    def _deps(self, stream, sem_self, reads, writes):
        deps = {}
        def add(d):
            if d is not None and deps.get(d[0], 0) < d[1]:
                deps[d[0]] = d[1]
        for k in reads:
            add(self.lastw.get(k))
        for k in writes:
            add(self.lastw.get(k))
            for d in self.readers.get(k, ()):
                add(d)
        waits = []
        seen = self.seen[stream]
        for s, c in deps.items():
            if s == sem_self and stream == "pe":
                continue
            if seen.get(s, 0) < c:
                seen[s] = c
                waits.append((s, c))
        return waits

    def _commit(self, sem, reads, writes):
        self.cnt[sem] += 1
        tag = (sem, self.cnt[sem])
        for k in reads:
            self.readers.setdefault(k, []).append(tag)
        for k in writes:
            self.lastw[k] = tag
            self.readers[k] = []

    class _Rec:
        def __init__(self):
            self.calls = []

        def __getattr__(self, name):
            def f(*a, **k):
                import sys
                self.calls.append((name, a, k, sys._getframe(1).f_lineno))
                return None
            return f

    def op(self, stream, fn, reads=(), writes=()):
        reads, writes = self._flat(reads), self._flat(writes)
        sem = "c_" + stream
        rec = Prog._Rec()
        fn(rec)
        calls = rec.calls
        assert calls
        waits = self._deps(stream, sem, reads, writes)
        if stream == "pe":
            self.ops[stream].append((calls, waits, sem, "last"))
            self._commit(sem, reads, writes)
        else:
            self.ops[stream].append((calls, waits, sem, "each"))
            n = len(calls)
            self.cnt[sem] += n - 1
            self._commit(sem, reads, writes)
            self.seen[stream][sem] = max(self.seen[stream].get(sem, 0), self.cnt[sem] - 1)
        self.n_ops += 1

    def dma(self, stream, out, in_, reads=(), writes=(), **kw):
        reads, writes = self._flat(reads), self._flat(writes)
        stream = "sp"
        i = self.dma_i[stream]
        self.dma_i[stream] += 1
        sem = "d_%s%d" % (stream, i % self.NSLOT)
        waits = self._deps(stream, sem, reads, writes)
        prev = self.cnt[sem]
        if prev > 0 and self.seen[stream].get(sem, 0) < prev:
            self.seen[stream][sem] = prev
            waits.append((sem, prev))
        self.ops[stream].append(([("dma_start", (), dict(out=out, in_=in_, **kw), 0)], waits, sem, "dma"))
        self._commit(sem, reads, writes)
        self.n_ops += 1

    def barrier(self):
        for s in STREAMS:
            waits = []
            for n in self.sem_names:
                c = self.cnt[n]
                if self.seen[s].get(n, 0) < c:
                    self.seen[s][n] = c
                    waits.append((n, c))
            self.ops[s].append((None, waits, None, None))

    def end(self):
        self.barrier()
        prog = self
        nc = self.nc
        if self.dbg is not None:
            print("phase end: sbuf remaining", nc.sbuf_bytes_remaining)
        def run(stream, eng):
            base = dict(prog.emitted)
            for calls, waits, sem, mode in prog.ops[stream]:
                for s_, c in waits:
                    eng.wait_ge(prog.sems[s_], c * (16 if s_[0] == "d" else 1))
                if calls is None:
                    continue
                n = len(calls)
                for i, (name, a, k, lineno) in enumerate(calls):
                    if mode == "each" and i > 0:
                        eng.wait_ge(prog.sems[sem], prog.emitted[sem])
                    ins = getattr(eng, name)(*a, **k)
                    if prog.dbg is not None:
                        prog.dbg.append((str(ins), lineno))
                    if mode == "each":
                        ins.then_inc(prog.sems[sem], 1)
                        prog.emitted[sem] += 1
                    elif mode == "last":
                        if i == n - 1:
                            ins.then_inc(prog.sems[sem], 1)
                            prog.emitted[sem] += 1
                    else:
                        ins.then_inc(prog.sems[sem], 16)
                        prog.emitted[sem] += 1
        with nc.Block() as block:
            @block.sync
            def _(e):
                run("sp", e)
            @block.scalar
            def _(e):
                run("act", e)
            @block.vector
            def _(e):
                run("dve", e)
            @block.gpsimd
            def _(e):
                run("pool", e)
            @block.tensor
            def _(e):
                run("pe", e)
        self.ops = {s: [] for s in STREAMS}
        self.pstack.close()
        self.pstack = None

    def close(self):
        if self.dbg is not None:
            print("final sem counts", self.cnt)
        self.gstack.close()


def build_program(debug=False, stop=None, only_job=None, depth=DEPTH, dump=(), stop_l=0, start_l=0):
    nc = bass.Bass("TRN2", target_bir_lowering=False)

    def din(name, shape):
        return nc.dram_tensor(name, list(shape), F32, kind="ExternalInput").ap()

    def dout(name, shape):
        return nc.dram_tensor(name, list(shape), F32, kind="ExternalOutput").ap()

    def dscr(name, shape, dt):
        if name in dump:
            return nc.dram_tensor(name, list(shape), dt, kind="ExternalOutput").ap()
        return nc.dram_tensor(name, list(shape), dt).ap()

    x_s = din("x_s", [TS, D])
    x_p = din("x_p", [TP, D])
    cache_k = din("cache_k", [DEPTH, NCTX, 256])
    cache_v = din("cache_v", [DEPTH, NCTX, 256])
    state_C = din("state_C", [DEPTH, 2, 4, 256, 256])
    state_n = din("state_n", [DEPTH, 2, 4, 256])
    state_m = din("state_m", [DEPTH, 8])
    cond = din("cond", [2, D])
    w_mod = din("w_mod", [DEPTH, D, 6 * D])
    b_mod = din("b_mod", [DEPTH, 6 * D])
    n_pre1 = din("norm_pre1", [DEPTH, D])
    n_post1 = din("norm_post1", [DEPTH, D])
    n_pre2 = din("norm_pre2", [DEPTH, D])
    n_post2 = din("norm_post2", [DEPTH, D])
    w_in = din("w_in", [DEPTH, D, INW])
    q_norm = din("q_norm", [DEPTH, 128])
    k_norm = din("k_norm", [DEPTH, 128])
    gate_bias = din("mlstm_gate_bias", [DEPTH, 16])
    m_norm = din("mlstm_norm", [DEPTH, 1024])
    conv_w = din("conv_w", [DEPTH, 3, 1024])
    w_branch = din("w_branch", [DEPTH, 3, 1024, D])
    w_out = din("w_out", [DEPTH, D, D])
    w_ffn_in = din("w_ffn_in", [DEPTH, D, 2 * FF])
    w_ffn_out = din("w_ffn_out", [DEPTH, FF, D])
    cosT = din("cosT", [TS + TP, 64])
    sinT = din("sinT", [TS + TP, 64])
    cmask = din("cmask", [4, 128, 128])

    y_s = dout("y_s", [TS, D])
    y_p = dout("y_p", [TP, D])
    o_k = dout("o_k", [DEPTH, TP, 256])
    o_v = dout("o_v", [DEPTH, TP, 256])
    o_C = dout("o_C", [2, DEPTH, 2, 4, 256, 256])
    o_n = dout("o_n", [2, DEPTH, 2, 4, 256])
    o_m = dout("o_m", [2, DEPTH, 8])

    MOD = dscr("MOD", [DEPTH, 2, 6 * D], F32)
    XS1 = dscr("XS1", [TS, D], F32)
    XP1 = dscr("XP1", [TP, D], F32)
    XMID = dscr("XMID", [TS, D], F32)
    MODC = dscr("MODC", [3, D], F32)
    JK = dscr("JK", [TS, 256], F32)
    JV = dscr("JV", [TS, 256], F32)
    JC = dscr("JC", [2, 4, 256, 256], F32)
    JN = dscr("JN", [2, 4, 256], F32)
    JM = dscr("JM", [8], F32)
    NKS = NCTX + TS
    QT = dscr("QT", [TS // 128, 128, 1024], BF16)
    KT = dscr("KT", [128, 2, NKS], BF16)
    VV = dscr("VV", [NKS, 256], BF16)
    MQT = dscr("MQT", [TS // 128, 128, 1024], BF16)
    MKT = dscr("MKT", [TS // 128, 128, 1024], BF16)
    MK = dscr("MK", [TS, 1024], BF16)
    MV = dscr("MV", [TS, 1024], BF16)
    MOS = dscr("MOS", [TS, 1024], BF16)
    MG = dscr("MG", [TS, 16], F32)
    CB = dscr("CB", [128, 8, TS], BF16)
    CU = dscr("CU", [128, 8, TS], BF16)
    GS = dscr("GS", [128, 48, TS], BF16)
    ATT = dscr("ATT", [128, 8, TS], BF16)
    MLT = dscr("MLT", [128, 8, TS], BF16)

    P = Prog(nc)
    if debug:
        P.dbg = []
        nc._dbg = P.dbg
    ident = P.gsb("ident", [128, 128], BF16)
    identf = P.gsb("identf", [128, 128], F32)
    onesf = P.gsb("onesf", [128, 128], F32)
    onesb = P.gsb("onesb", [128, 128], BF16)
    cm = P.gsb("cm", [128, 4, 128], F32)
    pb = [P.gps("pb%d" % i, [128, 512], F32) for i in range(7)]
    ptb = P.gps("ptb", [128, 1024], BF16)
    PBK = [["pb%da" % i, "pb%db" % i] for i in range(7)]

    wstate = {"i": 0}

    class WPool:
        def __init__(self):
            self.st = [P.sb("wst%d" % i, [128, WK, WC], F32) for i in range(2)]
            self.bf = [P.sb("wbf%d" % i, [128, WK, WC], BF16) for i in range(3)]
            self.i = 0

        def block(self, src, k0, nk, c0, ncols):
            i = self.i
            self.i += 1
            st = self.st[i % 2]
            bf = self.bf[i % 3]
            ks, kb = "wst%d" % (i % 2), "wbf%d" % (i % 3)
            P.dma("sp", st[:, 0:nk, 0:ncols],
                  src[k0 * 128:(k0 + nk) * 128, c0:c0 + ncols].rearrange("(kc p) n -> p kc n", p=128),
                  writes=[ks])
            P.op("pool" if i % 3 == 0 else "dve",
                 lambda e, st=st, bf=bf, nk=nk, ncols=ncols: e.tensor_copy(out=bf[:, 0:nk, 0:ncols], in_=st[:, 0:nk, 0:ncols]),
                 reads=[ks], writes=[kb])
            return bf, kb

    def lin_fm(W, src, KCtot, c0, ncols, rhs, rhs_key, ntok, consume, wp, bank0=0):
        nblk = (ncols + WC - 1) // WC
        for bi in range(nblk):
            cc0 = c0 + bi * WC
            ncb = min(WC, c0 + ncols - cc0)
            nm = (ncb + 127) // 128
            banks = [bank0 + (2 * (bi % 2) + m) for m in range(nm)]
            nkh = (KCtot + WK - 1) // WK
            for kh in range(nkh):
                nk = min(WK, KCtot - kh * WK)
                bf, kb = wp.block(src, kh * WK, nk, cc0, ncb)
                for m in range(nm):
                    mw = min(128, ncb - m * 128)
                    bk = banks[m]
                    def f(e, bf=bf, kh=kh, nk=nk, m=m, mw=mw, bk=bk, nkh=nkh):
                        ins = None
                        for kc in range(nk):
                            ins = e.matmul(pb[bk][0:mw, 0:ntok], lhsT=bf[:, kc, m * 128:m * 128 + mw],
                                           rhs=rhs[:, kh * WK + kc, 0:ntok],
                                           start=(kh == 0 and kc == 0), stop=(kh == nkh - 1 and kc == nk - 1))
                        return ins
                    P.op("pe", f, reads=[kb, rhs_key], writes=PBK[bk])
            for m in range(nm):
                mw = min(128, ncb - m * 128)
                consume((cc0 - c0) // 128 + m, pb[banks[m]][0:mw, 0:ntok], PBK[banks[m]])

    def lin_tm(W, src, KCtot, c0, ncols, lhs, lhs_key, ntb, consume, wp):
        nblk = (ncols + WC - 1) // WC
        for bi in range(nblk):
            cc0 = c0 + bi * WC
            ncb = min(WC, c0 + ncols - cc0)
            nkh = (KCtot + WK - 1) // WK
            half = (bi % 2) * 256
            for kh in range(nkh):
                nk = min(WK, KCtot - kh * WK)
                bf, kb = wp.block(src, kh * WK, nk, cc0, ncb)
                for tb in range(ntb):
                    def f(e, bf=bf, kh=kh, nk=nk, tb=tb, ncb=ncb, half=half, nkh=nkh):
                        ins = None
                        for kc in range(nk):
                            ins = e.matmul(pb[tb][:, half:half + ncb], lhsT=lhs[:, kh * WK + kc, tb * 128:(tb + 1) * 128],
                                           rhs=bf[:, kc, 0:ncb],
                                           start=(kh == 0 and kc == 0), stop=(kh == nkh - 1 and kc == nk - 1))
                        return ins
                    P.op("pe", f, reads=[kb, lhs_key], writes=[PBK[tb][0 if half == 0 else 1]])
            for tb in range(ntb):
                consume(tb, cc0 - c0, ncb, pb[tb][:, half:half + ncb], [PBK[tb][0 if half == 0 else 1]])

    P.begin()
    def mk_ident(e):
        e.memset(identf[:], 0.0)
        return e.affine_select(out=identf[:], in_=identf[:], pattern=[[-1, 128]], compare_op=ALU.not_equal,
                               fill=1.0, base=0, channel_multiplier=1)
    P.op("pool", mk_ident, writes=["identf"])
    P.op("pool", lambda e: e.tensor_copy(out=ident[:], in_=identf[:]), reads=["identf"], writes=["ident"])
    P.op("pool", lambda e: e.memset(onesf[:], 1.0), writes=["onesf"])
    P.op("pool", lambda e: e.memset(onesb[:], 1.0), writes=["onesb"])
    P.dma("sp", cm[:], cmask.rearrange("a p n -> p a n"), writes=["cm"])
    maskF, maskB, triF, triB = cm[:, 0, :], cm[:, 1, :], cm[:, 2, :], cm[:, 3, :]

    wp = WPool()
    cnd = P.sb("cnd", [128, 2, 16], F32)
    cndb = P.sb("cndb", [128, 16, 2], BF16)
    for ci in range(2):
        P.dma("act", cnd[:, ci, :], cond[ci].rearrange("(kc p) -> p kc", p=128), writes=["cnd"],
              allow_slow_non_contiguous=True)
    P.op("act", lambda e: e.activation(out=cndb[:].rearrange("p k c -> p c k"), in_=cnd[:], func=AF.Silu),
         reads=["cnd"], writes=["cndb"])
    bm = [P.sb("bm%d" % i, [2, WC], F32) for i in range(2)]
    mo_ = [P.sb("mo%d" % i, [2, WC], F32) for i in range(2)]
    for l in range(DEPTH):
        nblk = 6 * D // WC
        for bi in range(nblk):
            c0 = bi * WC
            j = bi % 2
            P.dma("act", bm[j][:], b_mod[l:l + 1, c0:c0 + WC].partition_broadcast(2) if False else
                  b_mod[l, c0:c0 + WC].partition_broadcast(2), writes=["bm%d" % j])
            bk = 4 + j
            for kh in range(2):
                bf, kb = wp.block(w_mod[l], kh * WK, WK, c0, WC)
                def f(e, bf=bf, kh=kh, bk=bk):
                    ins = None
                    for kc in range(WK):
                        ins = e.matmul(pb[bk][0:2, 0:WC], lhsT=cndb[:, kh * WK + kc, :], rhs=bf[:, kc, :],
                                       start=(kh == 0 and kc == 0), stop=(kh == 1 and kc == WK - 1))
                    return ins
                P.op("pe", f, reads=[kb, "cndb"], writes=PBK[bk])
            P.op("dve", lambda e, j=j, bk=bk: e.tensor_tensor(out=mo_[j][:], in0=pb[bk][0:2, 0:WC], in1=bm[j][:], op=ALU.add),
                 reads=[PBK[bk], "bm%d" % j], writes=["mo%d" % j])
            P.dma("act", MOD[l, :, c0:c0 + WC], mo_[j][:], reads=["mo%d" % j], writes=["MOD"])
    P.end()

    def load_row_bc(dst, dst_key, row_ap, stream="act", extra_reads=()):
        P.dma(stream, dst, row_ap.partition_broadcast(128), reads=list(extra_reads), writes=[dst_key])

    def act_rstd(e, rstd, ss, n):
        e.activation(out=rstd, in_=ss, func=AF.Ln, scale=1.0 / n, bias=EPS)
        return e.activation(out=rstd, in_=rstd, func=AF.Exp, scale=-0.5)

    jobs = [
        dict(name="S", T=TS, ci=1, seqs=[(0, TS)], nctx=NCTX, x_in=x_s, x_l1=XS1, y=y_s, rope0=0),
        dict(name="P", T=TP, ci=0, seqs=[(0, 256), (256, 256)], nctx=0, x_in=x_p, x_l1=XP1, y=y_p, rope0=TS),
    ]

    if stop == "0":
        P.close()
        return nc
    for l in range(start_l, depth):
        for job in jobs:
            if only_job is not None and job["name"] != only_job:
                continue
            T = job["T"]
            ci = job["ci"]
            ntile = T // 512
            x_src = job["x_in"] if l == 0 else job["x_l1"]
            x_dst = job["x_l1"] if l == 0 else job["y"]
            isP = job["name"] == "P"
            nctx = job["nctx"]
            NK = nctx + T
            P.begin()
            wp = WPool()
            A1 = P.sb("A1", [128, D], F32)
            B1 = P.sb("B1", [128, D], F32)
            load_row_bc(A1[:], "A1", MOD[l, ci, 1 * D:2 * D], extra_reads=["MOD"])
            load_row_bc(B1[:], "B1", n_pre1[l])
            P.op("dve", lambda e: e.scalar_tensor_tensor(out=A1[:], in0=A1[:], scalar=1.0, in1=B1[:], op0=ALU.add, op1=ALU.mult),
                 reads=["A1", "B1"], writes=["A1"])
            load_row_bc(B1[:], "B1", MOD[l, ci, 0:D], extra_reads=["MOD"])
            gq = P.sb("gq", [128, 128], F32)
            gk = P.sb("gk", [128, 128], F32)
            gb = P.sb("gb", [128, 16], F32)
            load_row_bc(gq[:], "gq", q_norm[l])
            load_row_bc(gk[:], "gk", k_norm[l])
            load_row_bc(gb[:], "gb", gate_bias[l])
            P.op("dve", lambda e: e.tensor_scalar(out=gq[:], in0=gq[:], scalar1=128.0 ** -0.5, scalar2=None, op0=ALU.mult),
                 reads=["gq"], writes=["gq"])
            xs = [P.sb("xs%d" % i, [128, D], F32) for i in range(2)]
            htm = [P.sb("htm%d" % i, [128, D], BF16) for i in range(2)]
            junk = P.sb("junk", [128, D], BF16)
            tmpf = P.sb("tmpf", [128, D], F32)
            ss = P.sb("ss", [128, 8], F32)
            hT = P.sb("hT", [128, 16, 512], BF16)
            cs = P.sb("cs", [128, 4, 64], F32)
            sn_ = P.sb("sn", [128, 4, 64], F32)
            ev = [P.sb("ev%d" % i, [128, WC], F32) for i in range(2)]
            ev2 = [P.sb("evb%d" % i, [128, WC], F32) for i in range(2)]
            evr = [P.sb("evr%d" % i, [128, WC], BF16) for i in range(2)]
            r1 = P.sb("r1", [128, WC], F32)
            r2 = P.sb("r2", [128, WC], F32)
            s4 = P.sb("s4", [128, 4], F32)
            obf = [P.sb("obf%d" % i, [128, 512], BF16) for i in range(3)]
            og = [P.sb("og%d" % i, [128, 16], F32) for i in range(2)]
            QTs = P.sb("QTs", [128, 4, 1024], BF16)
            KTs = P.sb("KTs", [128, 2, 512], BF16)
            MQs = P.sb("MQs", [128, 4, 1024], BF16)
            MKs = P.sb("MKs", [128, 4, 1024], BF16)
            evc = {"i": 0}
            ccs_t = [P.sb("ccs%d" % i, [128, 2, 512], F32) for i in range(2)]

            if nctx:
                for kb_ in range(nctx // 128):
                    t = P.sb("ck%d" % kb_, [128, 256], F32)
                    tb_ = P.sb("ckb%d" % kb_, [128, 256], BF16)
                    tv = P.sb("cv%d" % kb_, [128, 256], F32)
                    tvb = P.sb("cvb%d" % kb_, [128, 256], BF16)
                    kT_ = P.sb("ckT%d" % kb_, [128, 2, 128], BF16)
                    P.dma("act", t[:], cache_k[l, kb_ * 128:(kb_ + 1) * 128, :], writes=["ck%d" % kb_])
                    P.dma("act", tv[:], cache_v[l, kb_ * 128:(kb_ + 1) * 128, :], writes=["cv%d" % kb_])
                    P.op("dve", lambda e, t=t, tb_=tb_: e.tensor_copy(out=tb_[:], in_=t[:]), reads=["ck%d" % kb_], writes=["ckb%d" % kb_])
                    P.op("dve", lambda e, tv=tv, tvb=tvb: e.tensor_copy(out=tvb[:], in_=tv[:]), reads=["cv%d" % kb_], writes=["cvb%d" % kb_])
                    def ftr(e, tb_=tb_):
                        ins = None
                        for h in range(2):
                            ins = e.transpose(ptb[:, h * 128:(h + 1) * 128], tb_[:, h * 128:(h + 1) * 128], ident[:])
                        return ins
                    P.op("pe", ftr, reads=["ckb%d" % kb_, "ident"], writes=["ptb"])
                    P.op("act", lambda e, kT_=kT_: e.copy(out=kT_[:].rearrange("p h t -> p (h t)"), in_=ptb[:, 0:256]),
                         reads=["ptb"], writes=["ckT%d" % kb_])
                    P.dma("act", KT[:, :, kb_ * 128:(kb_ + 1) * 128], kT_[:], reads=["ckT%d" % kb_], writes=["KT"])
                    P.dma("act", VV[kb_ * 128:(kb_ + 1) * 128, :], tvb[:], reads=["cvb%d" % kb_], writes=["VV"])

            for ti in range(ntile):
                t0 = ti * 512
                P.dma("act", cs[:], cosT[job["rope0"] + t0:job["rope0"] + t0 + 512, :].rearrange("(tb p) f -> p tb f", p=128), writes=["cs"])
                P.dma("act", sn_[:], sinT[job["rope0"] + t0:job["rope0"] + t0 + 512, :].rearrange("(tb p) f -> p tb f", p=128), writes=["sn"])
                for tb in range(4):
                    j = tb % 2
                    xk, hk = "xs%d" % j, "htm%d" % j
                    P.dma("sp", xs[j][:], x_src[t0 + tb * 128:t0 + (tb + 1) * 128, :], writes=[xk])
                    def fsq(e, j=j, tb=tb):
                        e.memzero(ss[:, tb:tb + 1])
                        e.activation(out=junk[:], in_=xs[j][:], func=AF.Square, accum_out=ss[:, tb:tb + 1])
                        return act_rstd(e, ss[:, 4 + tb:5 + tb], ss[:, tb:tb + 1], D)
                    P.op("act", fsq, reads=[xk], writes=["junk", "ss%d" % tb])
                    def fn1(e, j=j, tb=tb):
                        e.scalar_tensor_tensor(out=tmpf[:], in0=xs[j][:], scalar=ss[:, 4 + tb:5 + tb], in1=A1[:], op0=ALU.mult, op1=ALU.mult)
                        return e.tensor_tensor(out=htm[j][:], in0=tmpf[:], in1=B1[:], op=ALU.add)
                    P.op("dve", fn1, reads=[xk, "ss%d" % tb, "A1", "B1"], writes=["tmpf", hk])
                    for g in range(2):
                        def ftr(e, j=j, g=g):
                            ins = None
                            for kk in range(8):
                                kc = g * 8 + kk
                                ins = e.transpose(ptb[:, kk * 128:(kk + 1) * 128], htm[j][:, kc * 128:(kc + 1) * 128], ident[:])
                            return ins
                        P.op("pe", ftr, reads=[hk, "ident"], writes=["ptb"])
                        P.op("act", lambda e, g=g, tb=tb: e.copy(out=hT[:, g * 8:(g + 1) * 8, tb * 128:(tb + 1) * 128],
                                                                 in_=ptb[:].rearrange("p (k t) -> p k t", k=8)),
                             reads=["ptb"], writes=["hT"])

                def nxt():
                    evc["i"] += 1
                    return evc["i"] % 2

                def qk_consume(kind):
                    def consume(tb, coff, ncb, ps, pkey):
                        nh = ncb // 128
                        j = nxt()
                        g = gq if kind == "q" else gk
                        gkey = "gq" if kind == "q" else "gk"
                        def f0(e, j=j):
                            e.tensor_copy(out=ev[j][:, 0:ncb], in_=ps)
                            e.tensor_tensor(out=r1[:, 0:ncb], in0=ev[j][:, 0:ncb], in1=ev[j][:, 0:ncb], op=ALU.mult)
                            return e.tensor_reduce(out=s4[:, 0:nh], in_=r1[:, 0:ncb].rearrange("p (h d) -> p h d", h=nh), axis=AX.X, op=ALU.add)
                        P.op("dve", f0, reads=[pkey], writes=["r1", "s4", "ev%d" % j])
                        P.op("act", lambda e: act_rstd(e, s4[:, 0:nh], s4[:, 0:nh], 128), reads=["s4"], writes=["s4"])
                        def f(e, j=j):
                            e.tensor_tensor(out=ev[j][:, 0:ncb].rearrange("p (h d) -> p h d", h=nh), in0=ev[j][:, 0:ncb].rearrange("p (h d) -> p h d", h=nh),
                                            in1=s4[:, 0:nh].unsqueeze(2).to_broadcast([128, nh, 128]), op=ALU.mult)
                            return e.tensor_tensor(out=ev[j][:, 0:ncb].rearrange("p (h d) -> p h d", h=nh), in0=ev[j][:, 0:ncb].rearrange("p (h d) -> p h d", h=nh),
                                                   in1=g[:].unsqueeze(1).to_broadcast([128, nh, 128]), op=ALU.mult)
                        P.op("dve", f, reads=["s4", gkey, "ev%d" % j], writes=["ev%d" % j])
                        if kind == "k":
                            dstk = (o_k[l] if isP else JK)
                            P.dma("act", dstk[t0 + tb * 128:t0 + (tb + 1) * 128, :], ev[j][:, 0:256], reads=["ev%d" % j], writes=["o_k"])
                        def frope(e, j=j, tb=tb):
                            xv = ev[j][:, 0:ncb].rearrange("p (h a x f) -> p h a x f", h=nh, a=2, x=2)
                            ov = evr[j][:, 0:ncb].rearrange("p (h a x f) -> p h a x f", h=nh, a=2, x=2)
                            t1v = r1[:, 0:ncb // 2].rearrange("p (h a f) -> p h a f", h=nh, a=2)
                            t2v = r2[:, 0:ncb // 2].rearrange("p (h a f) -> p h a f", h=nh, a=2)
                            cb_ = cs[:, tb, :].rearrange("p (a f) -> p a f", a=2).unsqueeze(1).to_broadcast([128, nh, 2, 32])
                            sb_ = sn_[:, tb, :].rearrange("p (a f) -> p a f", a=2).unsqueeze(1).to_broadcast([128, nh, 2, 32])
                            x1, x2 = xv[:, :, :, 0, :], xv[:, :, :, 1, :]
                            e.tensor_tensor(out=t1v, in0=x1, in1=cb_, op=ALU.mult)
                            e.tensor_tensor(out=t2v, in0=x2, in1=sb_, op=ALU.mult)
                            e.tensor_tensor(out=ov[:, :, :, 0, :], in0=t1v, in1=t2v, op=ALU.subtract)
                            e.tensor_tensor(out=t1v, in0=x2, in1=cb_, op=ALU.mult)
                            e.tensor_tensor(out=t2v, in0=x1, in1=sb_, op=ALU.mult)
                            return e.tensor_tensor(out=ov[:, :, :, 1, :], in0=t1v, in1=t2v, op=ALU.add)
                        P.op("dve", frope, reads=["ev%d" % j, "cs", "sn"], writes=["r1", "r2", "evr%d" % j])
                        def ftr(e, j=j):
                            ins = None
                            for h in range(nh):
                                ins = e.transpose(ptb[:, h * 128:(h + 1) * 128], evr[j][:, h * 128:(h + 1) * 128], ident[:])
                            return ins
                        P.op("pe", ftr, reads=["evr%d" % j, "ident"], writes=["ptb"])
                        if kind == "q":
                            h0 = coff // 128
                            P.op("act", lambda e, tb=tb, h0=h0: e.copy(out=QTs[:, tb, h0 * 128:h0 * 128 + ncb], in_=ptb[:, 0:ncb]),
                                 reads=["ptb"], writes=["QTs"])
                        else:
                            P.op("act", lambda e, tb=tb: e.copy(out=KTs[:, :, tb * 128:(tb + 1) * 128], in_=ptb[:, 0:256].rearrange("p (h t) -> p h t", h=2)),
                                 reads=["ptb"], writes=["KTs"])
                    return consume

                lin_tm(w_in, w_in[l], 16, SEC["aq"], 1024, hT, "hT", 4, qk_consume("q"), wp)
                lin_tm(w_in, w_in[l], 16, SEC["ak"], 256, hT, "hT", 4, qk_consume("k"), wp)

                def v_consume(tb, coff, ncb, ps, pkey):
                    j = nxt()
                    P.op("act", lambda e, j=j: e.copy(out=ev2[j][:], in_=ps), reads=[pkey], writes=["evb%d" % j])
                    P.op("pool", lambda e, j=j: e.tensor_copy(out=obf[j][:, 0:256], in_=ev2[j][:]), reads=["evb%d" % j], writes=["obf%d" % j])
                    dstv = (o_v[l] if isP else JV)
                    P.dma("act", dstv[t0 + tb * 128:t0 + (tb + 1) * 128, :], ev2[j][:], reads=["evb%d" % j], writes=["o_v"])
                    P.dma("act", VV[nctx + t0 + tb * 128:nctx + t0 + (tb + 1) * 128, :], obf[j][:, 0:256], reads=["obf%d" % j], writes=["VV"])
                lin_tm(w_in, w_in[l], 16, SEC["av"], 256, hT, "hT", 4, v_consume, wp)
                P.dma("act", QT[t0 // 128:t0 // 128 + 4].rearrange("tb p n -> p tb n"), QTs[:], reads=["QTs"], writes=["QT"])
                P.dma("act", KT[:, :, nctx + t0:nctx + t0 + 512], KTs[:], reads=["KTs"], writes=["KT"])

                def mqk_consume(dst, dkey, scale):
                    def consume(mi, ps, pkey):
                        h, dkc = mi // 2, mi % 2
                        def f(e):
                            return e.mul(dst[:].rearrange("p tb (h c t) -> p tb h c t", h=4, c=2)[:, :, h, dkc, :],
                                         ps.rearrange("p (tb t) -> p tb t", tb=4), scale)
                        P.op("act", f, reads=[pkey], writes=[dkey])
                    return consume
                lin_fm(w_in, w_in[l], 16, SEC["mq"], 1024, hT, "hT", 512, mqk_consume(MQs, "MQs", 1.0), wp, bank0=0)
                lin_fm(w_in, w_in[l], 16, SEC["mk"], 1024, hT, "hT", 512, mqk_consume(MKs, "MKs", 1.0 / 16.0), wp, bank0=0)
                P.dma("act", MQT[t0 // 128:t0 // 128 + 4].rearrange("tb p n -> p tb n"), MQs[:], reads=["MQs"], writes=["MQT"])
                P.dma("act", MKT[t0 // 128:t0 // 128 + 4].rearrange("tb p n -> p tb n"), MKs[:], reads=["MKs"], writes=["MKT"])

                def tm_consume(dstD, dkey, func, scale):
                    def consume(tb, coff, ncb, ps, pkey):
                        j = evc["i"] % 3
                        evc["i"] += 1
                        P.op("act", lambda e, j=j: e.activation(out=obf[j][:, 0:ncb], in_=ps, func=func, scale=scale),
                             reads=[pkey], writes=["obf%d" % j])
                        P.dma("act", dstD[t0 + tb * 128:t0 + (tb + 1) * 128, coff:coff + ncb], obf[j][:, 0:ncb], reads=["obf%d" % j], writes=[dkey])
                    return consume
                lin_tm(w_in, w_in[l], 16, SEC["mk"], 1024, hT, "hT", 4, tm_consume(MK, "MK", AF.Identity, 1.0 / 16.0), wp)
                lin_tm(w_in, w_in[l], 16, SEC["mv"], 1024, hT, "hT", 4, tm_consume(MV, "MV", AF.Identity, 1.0), wp)
                lin_tm(w_in, w_in[l], 16, SEC["mo"], 1024, hT, "hT", 4, tm_consume(MOS, "MOS", AF.Sigmoid, 1.0), wp)

                def g_consume(tb, coff, ncb, ps, pkey):
                    j = nxt()
                    P.op("dve", lambda e, j=j: e.tensor_tensor(out=og[j][:], in0=ps, in1=gb[:], op=ALU.add), reads=[pkey, "gb"], writes=["og%d" % j])
                    P.dma("act", MG[t0 + tb * 128:t0 + (tb + 1) * 128, :], og[j][:], reads=["og%d" % j], writes=["MG"])
                lin_tm(w_in, w_in[l], 16, SEC["mg"], 16, hT, "hT", 4, g_consume, wp)

                cct = P_cct = None
                def fm_store(dstD, dkey, ch0, func):
                    def consume(mi, ps, pkey):
                        j = evc["i"] % 3
                        evc["i"] += 1
                        P.op("act", lambda e, j=j: e.activation(out=obf[j][:], in_=ps, func=func), reads=[pkey], writes=["obf%d" % j])
                        P.dma("act", dstD[:, ch0 + mi, t0:t0 + 512], obf[j][:], reads=["obf%d" % j], writes=[dkey])
                    return consume
                lin_fm(w_in, w_in[l], 16, SEC["cb"], 1024, hT, "hT", 512, fm_store(CB, "CB", 0, AF.Identity), wp, bank0=0)
                for m2 in range(4):
                    ccs = ccs_t[m2 % 2]
                    def cc_consume(mi, ps, pkey, ccs=ccs, m2=m2):
                        P.op("act", lambda e, mi=mi: e.copy(out=ccs[:, mi, :], in_=ps), reads=[pkey], writes=["ccs%d_%d" % (m2 % 2, mi)])
                    def cx_consume(mi, ps, pkey, ccs=ccs, m2=m2):
                        j = evc["i"] % 3
                        evc["i"] += 1
                        P.op("dve", lambda e, j=j, mi=mi: e.tensor_tensor(out=obf[j][:], in0=ps, in1=ccs[:, mi, :], op=ALU.mult),
                             reads=[pkey, "ccs%d_%d" % (m2 % 2, mi)], writes=["obf%d" % j])
                        P.dma("act", CU[:, m2 * 2 + mi, t0:t0 + 512], obf[j][:], reads=["obf%d" % j], writes=["CU"])
                    lin_fm(w_in, w_in[l], 16, SEC["cc"] + m2 * 256, 256, hT, "hT", 512, cc_consume, wp, bank0=0)
                    lin_fm(w_in, w_in[l], 16, SEC["cx"] + m2 * 256, 256, hT, "hT", 512, cx_consume, wp, bank0=2)
                lin_fm(w_in, w_in[l], 16, SEC["gl"], 6144, hT, "hT", 512, fm_store(GS, "GS", 0, AF.Sigmoid), wp, bank0=0)
            P.end()
            if stop == "A" and l == stop_l:
                P.close()
                return nc

            P.begin()
            KTa = P.sb("KTa", [128, 2, NK], BF16)
            Va = P.sb("Va", [128, NK // 128, 256], BF16)
            P.dma("sp", KTa[:], KT[:, :, 0:NK], reads=["KT"], writes=["KTa"])
            P.dma("sp", Va[:], VV[0:NK, :].rearrange("(c p) n -> p c n", p=128), reads=["VV"], writes=["Va"])
            qb_ = [P.sb("qb%d" % i, [128, 1024], BF16) for i in range(2)]
            pt_ = [P.sb("pt%d" % i, [128, 512], BF16) for i in range(3)]
            rden = [P.sb("rden%d" % i, [128, 512], F32) for i in range(2)]
            ao = [P.sb("ao%d" % i, [128, 512], BF16) for i in range(2)]
            cnt = 0
            for (s0, sl) in job["seqs"]:
                kch = list(range(nctx // 128)) + [(nctx + s0) // 128 + i for i in range(sl // 128)]
                for qi in range(sl // 128):
                    qblk = s0 // 128 + qi
                    j = qblk % 2
                    P.dma("sp", qb_[j][:], QT[qblk], reads=["QT"], writes=["qb%d" % j])
                    for kvh in range(2):
                        bo, bd = (4, 5) if (cnt % 2 == 0) else (2, 3)
                        cnt += 1
                        for ii, kc in enumerate(kch):
                            bs = ii % 2
                            pj = ii % 3
                            P.op("pe", lambda e, j=j, kc=kc, kvh=kvh, bs=bs: e.matmul(pb[bs][:, :], lhsT=KTa[:, kvh, kc * 128:(kc + 1) * 128],
                                                                                      rhs=qb_[j][:, kvh * 512:(kvh + 1) * 512], start=True, stop=True),
                                 reads=["KTa", "qb%d" % j], writes=[PBK[bs]])
                            P.op("act", lambda e, bs=bs, pj=pj: e.activation(out=pt_[pj][:], in_=pb[bs][:, :], func=AF.Exp),
                                 reads=[PBK[bs]], writes=["pt%d" % pj])
                            def fpv(e, kc=kc, kvh=kvh, pj=pj, ii=ii, bo=bo, bd=bd, last=(ii == len(kch) - 1)):
                                e.matmul(pb[bo][:, :], lhsT=Va[:, kc, kvh * 128:(kvh + 1) * 128], rhs=pt_[pj][:], start=(ii == 0), stop=last)
                                return e.matmul(pb[bd][:, :], lhsT=onesb[:], rhs=pt_[pj][:], start=(ii == 0), stop=last)
                            P.op("pe", fpv, reads=["Va", "pt%d" % pj, "onesb"], writes=[PBK[bo], PBK[bd]])
                        jj = cnt % 2
                        P.op("dve", lambda e, jj=jj, bd=bd: e.reciprocal(out=rden[jj][:], in_=pb[bd][:, :]), reads=[PBK[bd]], writes=["rden%d" % jj])
                        P.op("dve", lambda e, jj=jj, bo=bo: e.tensor_tensor(out=ao[jj][:], in0=pb[bo][:, :], in1=rden[jj][:], op=ALU.mult),
                             reads=[PBK[bo], "rden%d" % jj], writes=["ao%d" % jj])
                        P.dma("act", ATT[:, kvh * 4:(kvh + 1) * 4, qblk * 128:(qblk + 1) * 128], ao[jj][:].rearrange("p (h t) -> p h t", h=4),
                              reads=["ao%d" % jj], writes=["ATT"])
            P.end()
            if stop == "B1" and l == stop_l:
                P.close()
                return nc

            for si, (s0, sl) in enumerate(job["seqs"]):
                P.begin()
                nch = sl // 128
                hacc = P.sb("hacc", [128, nch, 1024], F32)
                Cst = [[P.sb("C%d%d" % (d, h), [128, 2, 256], F32) for h in range(4)] for d in range(2)]
                Cb = [[P.sb("Cb%d%d" % (d, h), [128, 2, 257], BF16) for h in range(4)] for d in range(2)]
                nst = [P.sb("n%d" % d, [128, 4, 2], F32) for d in range(2)]
                mbc = P.sb("mbc", [128, 8], F32)
                gm = P.sb("gm", [128, 1024], F32)
                load_row_bc(gm[:], "gm", m_norm[l])
                for d in range(2):
                    for h in range(4):
                        if isP:
                            P.op("pool", lambda e, d=d, h=h: e.memset(Cst[d][h][:], 0.0), writes=["C%d%d" % (d, h)])
                        else:
                            P.dma("sp", Cst[d][h][:], state_C[l, d, h].rearrange("(c p) v -> p c v", p=128), writes=["C%d%d" % (d, h)])
                    if isP:
                        P.op("pool", lambda e, d=d: e.memset(nst[d][:], 0.0), writes=["n%d" % d])
                    else:
                        P.dma("sp", nst[d][:], state_n[l, d].rearrange("h (c p) -> p h c", p=128), writes=["n%d" % d], allow_slow_non_contiguous=True)
                    for h in range(4):
                        def fcb(e, d=d, h=h):
                            e.copy(out=Cb[d][h][:, :, 0:256], in_=Cst[d][h][:])
                            return e.copy(out=Cb[d][h][:, :, 256:257], in_=nst[d][:, h, :].unsqueeze(2))
                        P.op("act", fcb, reads=["C%d%d" % (d, h), "n%d" % d], writes=["Cb%d%d" % (d, h)])
                if isP:
                    P.op("pool", lambda e: e.memset(mbc[:], 0.0), writes=["mbc"])
                else:
                    load_row_bc(mbc[:], "mbc", state_m[l], stream="sp")

                NB = 2
                qTc = [P.sb("qTc%d" % i, [128, 4, 2, 128], BF16) for i in range(NB)]
                kTc = [P.sb("kTc%d" % i, [128, 4, 2, 128], BF16) for i in range(NB)]
                kc_ = [P.sb("kc%d" % i, [128, 1024], BF16) for i in range(NB)]
                va = [P.sb("va%d" % i, [128, 4, 257], BF16) for i in range(NB)]
                mg_ = [P.sb("mg%d" % i, [128, 16], F32) for i in range(NB)]
                for i in range(NB):
                    P.op("pool", lambda e, i=i: e.memset(va[i][:, :, 256:257], 1.0), writes=["va1_%d" % i])
                sp_ = P.sb("sp_", [128, 4], F32)
                ex_ = P.sb("ex_", [128, 4], F32)
                negb = P.sb("negb", [128, 8], F32)
                a_ = P.sb("a_", [128, 4], F32)
                diag = P.sb("diag", [128, 4, 128], F32)
                Am = P.sb("Am", [128, 4, 128], F32)
                Dm = P.sb("Dm", [128, 4, 128], F32)
                cmx = P.sb("cmx", [128, 4], F32)
                Mx = P.sb("Mx", [128, 4], F32)
                nM = P.sb("nM", [128, 4], F32)
                ML = P.sb("ML", [128, 4], F32)
                itr = P.sb("itr", [128, 4], F32)
                clp = P.sb("clp", [128, 4], F32)
                wv_ = P.sb("wv_", [128, 4], F32)
                dec = P.sb("dec", [128, 4], F32)
                t4 = P.sb("t4", [128, 4], F32)
                Sp = P.sb("Sp", [128, 4, 128], BF16)
                SpT = P.sb("SpT", [128, 4, 128], BF16)
                tI = [P.sb("tI%d" % i, [128, 257], F32) for i in range(2)]
                tot = [P.sb("tot%d" % i, [128, 257], F32) for i in range(2)]
                dn = [P.sb("dn%d" % i, [128, 2], F32) for i in range(2)]
                wvv = P.sb("wvv", [128, 4, 257], BF16)
                step = {"i": 0}

                def chunk_step(d, c):
                    i = step["i"] % NB
                    step["i"] += 1
                    ch = s0 // 128 + c
                    tok0 = s0 + c * 128
                    j0 = d * 4
                    P.dma("sp", qTc[i][:], MQT[ch].rearrange("p (h c t) -> p h c t", h=4, c=2), reads=["MQT"], writes=["qTc%d" % i])
                    P.dma("sp", kTc[i][:], MKT[ch].rearrange("p (h c t) -> p h c t", h=4, c=2), reads=["MKT"], writes=["kTc%d" % i])
                    P.dma("sp", kc_[i][:], MK[tok0:tok0 + 128, :], reads=["MK"], writes=["kc%d" % i])
                    P.dma("sp", va[i][:, :, 0:256], MV[tok0:tok0 + 128, :].rearrange("p (h v) -> p h v", h=4), reads=["MV"], writes=["va%d" % i])
                    P.dma("sp", mg_[i][:], MG[tok0:tok0 + 128, :], reads=["MG"], writes=["mg%d" % i])
                    ipre = mg_[i][:, d * 8:d * 8 + 4]
                    fpre = mg_[i][:, d * 8 + 4:d * 8 + 8]
                    tri = triF if d == 0 else triB
                    msk = maskF if d == 0 else maskB
                    def fsp(e):
                        e.activation(out=ex_[:], in_=fpre, func=AF.Exp, scale=-1.0)
                        return e.activation(out=sp_[:], in_=ex_[:], func=AF.Ln, bias=1.0)
                    P.op("act", fsp, reads=["mg%d" % i], writes=["ex_", "sp_"])
                    def fcs(e):
                        e.matmul(pb[6][:, 0:4], lhsT=tri, rhs=sp_[:], start=True, stop=True)
                        return e.matmul(pb[6][:, 4:8], lhsT=onesf[:], rhs=sp_[:], start=True, stop=True)
                    P.op("pe", fcs, reads=["sp_", "cm", "onesf"], writes=[PBK[6]])
                    def fa(e):
                        e.tensor_copy(out=negb[:], in_=pb[6][:, 0:8])
                        e.tensor_tensor(out=a_[:], in0=ipre, in1=negb[:, 0:4], op=ALU.add)
                        ins = None
                        for h in range(4):
                            ins = e.tensor_scalar(out=diag[:, h, :], in0=identf[:], scalar1=a_[:, h:h + 1], scalar2=None, op0=ALU.mult)
                        return ins
                    P.op("dve", fa, reads=[PBK[6], "mg%d" % i, "identf"], writes=["negb", "a_", "diag"])
                    def fbc(e):
                        ins = None
                        for h in range(4):
                            ins = e.matmul(pb[5][:, h * 128:(h + 1) * 128], lhsT=onesf[:], rhs=diag[:, h, :], start=True, stop=True)
                        return ins
                    P.op("pe", fbc, reads=["diag", "onesf"], writes=[PBK[5]])
                    def fm(e):
                        abc = pb[5][:, :].rearrange("p (h s) -> p h s", h=4)
                        e.tensor_reduce(out=ML[:], in_=abc, axis=AX.X, op=ALU.max)
                        e.tensor_tensor(out=Am[:], in0=abc, in1=msk.unsqueeze(1).to_broadcast([128, 4, 128]), op=ALU.add)
                        e.tensor_reduce(out=cmx[:], in_=Am[:], axis=AX.X, op=ALU.max)
                        e.tensor_tensor(out=Mx[:], in0=cmx[:], in1=mbc[:, j0:j0 + 4], op=ALU.max)
                        e.tensor_scalar(out=nM[:], in0=Mx[:], scalar1=-1.0, scalar2=None, op0=ALU.mult)
                        e.tensor_tensor(out=ML[:], in0=ML[:], in1=mbc[:, j0:j0 + 4], op=ALU.max)
                        e.tensor_tensor(out=itr[:], in0=mbc[:, j0:j0 + 4], in1=Mx[:], op=ALU.subtract)
                        e.tensor_tensor(out=clp[:], in0=negb[:, 0:4], in1=Mx[:], op=ALU.subtract)
                        e.tensor_tensor(out=wv_[:], in0=a_[:], in1=ML[:], op=ALU.subtract)
                        e.tensor_tensor(out=dec[:], in0=mbc[:, j0:j0 + 4], in1=ML[:], op=ALU.subtract)
                        return e.tensor_tensor(out=mbc[:, j0:j0 + 4], in0=ML[:], in1=negb[:, 4:8], op=ALU.subtract)
                    P.op("dve", fm, reads=[PBK[5], "cm", "mbc", "negb", "a_"], writes=["Am", "cmx", "Mx", "nM", "ML", "itr", "clp", "wv_", "dec", "mbc"])
                    def fex(e):
                        for h in range(4):
                            e.activation(out=Dm[:, h, :], in_=Am[:, h, :], func=AF.Exp, bias=nM[:, h:h + 1])
                        e.activation(out=itr[:], in_=itr[:], func=AF.Exp)
                        e.activation(out=clp[:], in_=clp[:], func=AF.Exp)
                        e.activation(out=wv_[:], in_=wv_[:], func=AF.Exp)
                        return e.activation(out=dec[:], in_=dec[:], func=AF.Exp)
                    P.op("act", fex, reads=["Am", "nM", "itr", "clp", "wv_", "dec"], writes=["Dm", "itr", "clp", "wv_", "dec"])
                    def fS(e):
                        ins = None
                        for h in range(4):
                            for c2 in range(2):
                                ins = e.matmul(pb[4][:, h * 128:(h + 1) * 128], lhsT=qTc[i][:, h, c2, :], rhs=kTc[i][:, h, c2, :],
                                               start=(c2 == 0), stop=(c2 == 1))
                        return ins
                    P.op("pe", fS, reads=["qTc%d" % i, "kTc%d" % i], writes=[PBK[4]])
                    P.op("dve", lambda e: e.tensor_tensor(out=Sp[:], in0=pb[4][:, :].rearrange("p (h s) -> p h s", h=4), in1=Dm[:], op=ALU.mult),
                         reads=[PBK[4], "Dm"], writes=["Sp"])
                    def fT(e):
                        ins = None
                        for h in range(4):
                            ins = e.transpose(ptb[:, h * 128:(h + 1) * 128], Sp[:, h, :], ident[:])
                        return ins
                    P.op("pe", fT, reads=["Sp", "ident"], writes=["ptb"])
                    P.op("act", lambda e: e.copy(out=SpT[:].rearrange("p h t -> p (h t)"), in_=ptb[:, 0:512]), reads=["ptb"], writes=["SpT"])
                    first = (c < nch // 2) == (d == 0)
                    for h in range(4):
                        jj = h % 2
                        bI, bL = (0, 1) if jj == 0 else (2, 3)
                        def fI(e, h=h, bI=bI, bL=bL):
                            for c2 in range(2):
                                e.matmul(pb[bI][:, 0:257], lhsT=qTc[i][:, h, c2, :], rhs=Cb[d][h][:, c2, :], start=(c2 == 0), stop=(c2 == 1))
                            return e.matmul(pb[bL][:, 0:257], lhsT=SpT[:, h, :], rhs=va[i][:, h, :], start=True, stop=True)
                        P.op("pe", fI, reads=["qTc%d" % i, "Cb%d%d" % (d, h), "SpT", "va%d" % i, "va1_%d" % i], writes=[PBK[bI], PBK[bL]])
                        P.op("act", lambda e, h=h, jj=jj, bI=bI: e.activation(out=tI[jj][:], in_=pb[bI][:, 0:257], func=AF.Identity, scale=itr[:, h:h + 1]),
                             reads=[PBK[bI], "itr"], writes=["tI%d" % jj])
                        def fh(e, h=h, jj=jj, bL=bL):
                            e.tensor_tensor(out=tot[jj][:], in0=tI[jj][:], in1=pb[bL][:, 0:257], op=ALU.add)
                            e.tensor_scalar(out=dn[jj][:, 0:1], in0=tot[jj][:, 256:257], scalar1=-1.0, scalar2=None, op0=ALU.mult)
                            e.tensor_tensor(out=dn[jj][:, 0:1], in0=dn[jj][:, 0:1], in1=tot[jj][:, 256:257], op=ALU.max)
                            e.tensor_tensor(out=dn[jj][:, 0:1], in0=dn[jj][:, 0:1], in1=clp[:, h:h + 1], op=ALU.max)
                            e.reciprocal(out=dn[jj][:, 1:2], in_=dn[jj][:, 0:1])
                            dst = hacc[:, c, h * 256:(h + 1) * 256]
                            if first:
                                return e.tensor_scalar(out=dst, in0=tot[jj][:, 0:256], scalar1=dn[jj][:, 1:2], scalar2=None, op0=ALU.mult)
                            return e.scalar_tensor_tensor(out=dst, in0=tot[jj][:, 0:256], scalar=dn[jj][:, 1:2], in1=dst, op0=ALU.mult, op1=ALU.add)
                        P.op("dve", fh, reads=["tI%d" % jj, PBK[bL], "clp"], writes=["tot%d" % jj, "dn%d" % jj, "hacc%d" % c])
                    P.op("dve", lambda e: e.tensor_tensor(out=wvv[:], in0=va[i][:], in1=wv_[:].unsqueeze(2).to_broadcast([128, 4, 257]), op=ALU.mult),
                         reads=["va%d" % i, "va1_%d" % i, "wv_"], writes=["wvv"])
                    for h in range(4):
                        for c2 in range(2):
                            bU = (h * 2 + c2) % 4
                            P.op("pe", lambda e, h=h, c2=c2, bU=bU: e.matmul(pb[bU][:, 0:257], lhsT=kc_[i][:, h * 256 + c2 * 128:h * 256 + (c2 + 1) * 128],
                                                                                rhs=wvv[:, h, :], start=True, stop=True),
                                 reads=["kc%d" % i, "wvv"], writes=[PBK[bU]])
                            def fu(e, h=h, c2=c2, bU=bU):
                                e.scalar_tensor_tensor(out=Cst[d][h][:, c2, :], in0=Cst[d][h][:, c2, :], scalar=dec[:, h:h + 1], in1=pb[bU][:, 0:256],
                                                       op0=ALU.mult, op1=ALU.add)
                                return e.scalar_tensor_tensor(out=nst[d][:, h, c2:c2 + 1], in0=nst[d][:, h, c2:c2 + 1], scalar=dec[:, h:h + 1],
                                                              in1=pb[bU][:, 256:257], op0=ALU.mult, op1=ALU.add)
                            P.op("dve", fu, reads=[PBK[bU], "dec", "C%d%d" % (d, h), "n%d" % d, "Cb%d%d" % (d, h)], writes=["C%d%d" % (d, h), "n%d" % d])
                        def fcb(e, h=h):
                            e.copy(out=Cb[d][h][:, :, 0:256], in_=Cst[d][h][:])
                            return e.copy(out=Cb[d][h][:, :, 256:257], in_=nst[d][:, h, :].unsqueeze(2))
                        P.op("act", fcb, reads=["C%d%d" % (d, h), "n%d" % d], writes=["Cb%d%d" % (d, h)])

                for stp in range(nch):
                    chunk_step(0, stp)
                    chunk_step(1, nch - 1 - stp)

                bsel = si
                for d in range(2):
                    for h in range(4):
                        dC = o_C[bsel, l, d, h] if isP else JC[d, h]
                        P.dma("act", dC.rearrange("(c p) v -> p c v", p=128), Cst[d][h][:], reads=["C%d%d" % (d, h)], writes=["o_C"])
                    dN = o_n[bsel, l, d] if isP else JN[d]
                    P.dma("act", dN.rearrange("h (c p) -> p h c", p=128), nst[d][:], reads=["n%d" % d], writes=["o_n"], allow_slow_non_contiguous=True)
                dM = o_m[bsel, l] if isP else JM
                P.dma("act", dM.unsqueeze(0), mbc[0:1, :], reads=["mbc"], writes=["o_m"])

                mos = [P.sb("mos%d" % i2, [128, 1024], BF16) for i2 in range(2)]
                hsq = P.sb("hsq", [128, 1024], F32)
                hs4 = P.sb("hs4", [128, 4], F32)
                hmb = [P.sb("hmb%d" % i2, [128, 1024], BF16) for i2 in range(2)]
                mlt = [P.sb("mlt%d" % i2, [128, 8, 128], BF16) for i2 in range(2)]
                for c in range(nch):
                    i2 = c % 2
                    tok0 = s0 + c * 128
                    P.dma("sp", mos[i2][:], MOS[tok0:tok0 + 128, :], reads=["MOS"], writes=["mos%d" % i2])
                    def fhn0(e, c=c):
                        hv = hacc[:, c, :]
                        e.tensor_tensor(out=hsq[:], in0=hv, in1=hv, op=ALU.mult)
                        return e.tensor_reduce(out=hs4[:], in_=hsq[:].rearrange("p (h v) -> p h v", h=4), axis=AX.X, op=ALU.add)
                    P.op("dve", fhn0, reads=["hacc%d" % c], writes=["hsq", "hs4"])
                    P.op("act", lambda e: act_rstd(e, hs4[:], hs4[:], 256), reads=["hs4"], writes=["hs4"])
                    def fhn(e, c=c, i2=i2):
                        hv = hacc[:, c, :]
                        e.tensor_tensor(out=hsq[:].rearrange("p (h v) -> p h v", h=4), in0=hv.rearrange("p (h v) -> p h v", h=4),
                                        in1=hs4[:].unsqueeze(2).to_broadcast([128, 4, 256]), op=ALU.mult)
                        e.tensor_tensor(out=hsq[:], in0=hsq[:], in1=gm[:], op=ALU.mult)
                        return e.tensor_tensor(out=hmb[i2][:], in0=hsq[:], in1=mos[i2][:], op=ALU.mult)
                    P.op("dve", fhn, reads=["hacc%d" % c, "gm", "mos%d" % i2, "hs4"], writes=["hsq", "hmb%d" % i2])
                    def fT2(e, i2=i2):
                        ins = None
                        for k in range(8):
                            ins = e.transpose(ptb[:, k * 128:(k + 1) * 128], hmb[i2][:, k * 128:(k + 1) * 128], ident[:])
                        return ins
                    P.op("pe", fT2, reads=["hmb%d" % i2, "ident"], writes=["ptb"])
                    P.op("act", lambda e, i2=i2: e.copy(out=mlt[i2][:].rearrange("p k t -> p (k t)"), in_=ptb[:, :]), reads=["ptb"], writes=["mlt%d" % i2])
                    P.dma("act", MLT[:, :, tok0:tok0 + 128], mlt[i2][:], reads=["mlt%d" % i2], writes=["MLT"])
                P.end()

            if stop == "B2" and l == stop_l:
                P.close()
                return nc
            P.begin()
            wp = WPool()
            mr = [P.sb("mr%d" % i, [128, D], F32) for i in range(2)]
            MODCk = "MODC_%d_%s" % (l, job["name"])
            def comb(idx, mod_off, nrm, add1):
                load_row_bc(mr[0][:], "mr0", MOD[l, ci, mod_off:mod_off + D], extra_reads=["MOD"])
                load_row_bc(mr[1][:], "mr1", nrm)
                if add1:
                    P.op("dve", lambda e: e.scalar_tensor_tensor(out=mr[0][:], in0=mr[0][:], scalar=1.0, in1=mr[1][:], op0=ALU.add, op1=ALU.mult),
                         reads=["mr0", "mr1"], writes=["mr0"])
                else:
                    P.op("dve", lambda e: e.tensor_tensor(out=mr[0][:], in0=mr[0][:], in1=mr[1][:], op=ALU.mult), reads=["mr0", "mr1"], writes=["mr0"])
                P.dma("act", MODC[idx:idx + 1, :], mr[0][0:1, :], reads=["mr0"], writes=[MODCk])
            comb(0, 2 * D, n_post1[l], False)
            comb(1, 4 * D, n_pre2[l], True)
            comb(2, 5 * D, n_post2[l], False)
            mrc = {"i": 0}
            def mrow(idx):
                j = mrc["i"] % 2
                mrc["i"] += 1
                if idx == 3:
                    load_row_bc(mr[j][:], "mr%d" % j, MOD[l, ci, 3 * D:4 * D], extra_reads=["MOD"])
                else:
                    load_row_bc(mr[j][:], "mr%d" % j, MODC[idx], extra_reads=[MODCk])
                return mr[j], "mr%d" % j
            cw = P.sb("cw", [128, 3, 8], F32)
            P.dma("act", cw[:], conv_w[l].rearrange("k (c p) -> p k c", p=128), writes=["cw"], allow_slow_non_contiguous=True)

            hT = P.sb("hT", [128, 16, 512], BF16)
            bins = [P.sb("bin%d" % g, [128, 8, 512], BF16) for g in range(3)]
            cut = [P.sb("cut%d" % i, [128, 514], BF16) for i in range(2)]
            cbt = [P.sb("cbt%d" % i, [128, 512], BF16) for i in range(2)]
            ctmp = P.sb("ctmp", [128, 512], F32)
            mtmp = [P.sb("mtmp%d" % i, [128, 512], F32) for i in range(2)]
            mix = P.sb("mix", [128, 4, D], BF16)
            ssq = P.sb("ssq", [128, 4, 8], F32)
            rs = P.sb("rs", [128, 8], F32)
            xs = [P.sb("xs%d" % i, [128, D], F32) for i in range(2)]
            tmpf = P.sb("tmpf", [128, D], F32)
            htm = [P.sb("htm%d" % i, [128, D], BF16) for i in range(2)]
            junk = P.sb("junk", [128, WC], F32)
            actT = P.sb("actT", [128, 44, 512], BF16)
            gsl = [P.sb("gsl%d" % i, [128, 3, 512], BF16) for i in range(2)]
            sgt = [P.sb("sg%d" % i, [128, 2, 512], F32) for i in range(2)]
            ec = {"i": 0}

            def transposes_to_hT(j, hk, tb):
                for g in range(2):
                    def ftr(e, j=j, g=g):
                        ins = None
                        for kk in range(8):
                            kc = g * 8 + kk
                            ins = e.transpose(ptb[:, kk * 128:(kk + 1) * 128], htm[j][:, kc * 128:(kc + 1) * 128], ident[:])
                        return ins
                    P.op("pe", ftr, reads=[hk, "ident"], writes=["ptb"])
                    P.op("act", lambda e, g=g, tb=tb: e.copy(out=hT[:, g * 8:(g + 1) * 8, tb * 128:(tb + 1) * 128],
                                                             in_=ptb[:].rearrange("p (k t) -> p k t", k=8)),
                         reads=["ptb"], writes=["hT"])

            for ti in range(ntile):
                t0 = ti * 512
                P.dma("sp", bins[0][:], ATT[:, :, t0:t0 + 512], reads=["ATT"], writes=["bin0"])
                P.dma("sp", bins[1][:], MLT[:, :, t0:t0 + 512], reads=["MLT"], writes=["bin1"])
                for ch in range(8):
                    j = ch % 2
                    lo_h = t0 - 1 if t0 - 1 >= 0 else t0
                    hi_h = t0 + 513 if t0 + 513 <= T else t0 + 512
                    P.dma("sp", cut[j][:, 1 + (lo_h - t0):1 + (hi_h - t0)], CU[:, ch, lo_h:hi_h], reads=["CU"], writes=["cut%d" % j])
                    P.dma("sp", cbt[j][:], CB[:, ch, t0:t0 + 512], reads=["CB"], writes=["cbt%d" % j])
                    for (s0, sl) in job["seqs"]:
                        lo, hi = max(s0, t0), min(s0 + sl, t0 + 512)
                        if lo >= hi:
                            continue
                        a, b = lo - t0, hi - t0
                        zl = (lo == s0)
                        zr = (hi == s0 + sl)
                        def fcv(e, ch=ch, j=j, a=a, b=b, zl=zl, zr=zr):
                            e.tensor_scalar(out=ctmp[:, a:b], in0=cut[j][:, 1 + a:1 + b], scalar1=cw[:, 1, ch:ch + 1], scalar2=None, op0=ALU.mult)
                            a1 = a + 1 if zl else a
                            e.scalar_tensor_tensor(out=ctmp[:, a1:b], in0=cut[j][:, a1:b], scalar=cw[:, 0, ch:ch + 1], in1=ctmp[:, a1:b],
                                                   op0=ALU.mult, op1=ALU.add)
                            b1 = b - 1 if zr else b
                            e.scalar_tensor_tensor(out=ctmp[:, a:b1], in0=cut[j][:, 2 + a:2 + b1], scalar=cw[:, 2, ch:ch + 1], in1=ctmp[:, a:b1],
                                                   op0=ALU.mult, op1=ALU.add)
                            return e.tensor_tensor(out=bins[2][:, ch, a:b], in0=ctmp[:, a:b], in1=cbt[j][:, a:b], op=ALU.mult)
                        P.op("dve", fcv, reads=["cut%d" % j, "cbt%d" % j, "cw"], writes=["ctmp", "bin2"])
                if stop == "C0" and l == stop_l:
                    P.end()
                    P.close()
                    return nc
                for bi in range(D // WC):
                    for g in range(3):
                        bf, kb = wp.block(w_branch[l, g], 0, 8, bi * WC, WC)
                        for m in range(2):
                            bk = g * 2 + m
                            def f(e, bf=bf, g=g, m=m, bk=bk):
                                ins = None
                                for kc in range(8):
                                    ins = e.matmul(pb[bk][:, :], lhsT=bf[:, kc, m * 128:(m + 1) * 128], rhs=bins[g][:, kc, :],
                                                   start=(kc == 0), stop=(kc == 7))
                                return ins
                            P.op("pe", f, reads=[kb, "bin%d" % g], writes=PBK[bk])
                    for m in range(2):
                        mi = bi * 2 + m
                        j = ec["i"] % 2
                        ec["i"] += 1
                        P.dma("act", gsl[j][:], GS[:, :, t0:t0 + 512].rearrange("p (g c) t -> p g c t", g=3)[:, :, mi, :], reads=["GS"], writes=["gsl%d" % j])
                        def fmg(e, j=j, m=m, mi=mi):
                            e.tensor_tensor(out=mtmp[0][:], in0=pb[0 + m][:, :], in1=gsl[j][:, 0, :], op=ALU.mult)
                            e.tensor_tensor(out=mtmp[1][:], in0=pb[2 + m][:, :], in1=gsl[j][:, 1, :], op=ALU.mult)
                            e.tensor_tensor(out=mtmp[0][:], in0=mtmp[0][:], in1=mtmp[1][:], op=ALU.add)
                            e.tensor_tensor(out=mtmp[1][:], in0=pb[4 + m][:, :], in1=gsl[j][:, 2, :], op=ALU.mult)
                            return e.tensor_tensor(out=hT[:, mi, :], in0=mtmp[0][:], in1=mtmp[1][:], op=ALU.add)
                        P.op("dve", fmg, reads=[PBK[m], PBK[2 + m], PBK[4 + m], "gsl%d" % j], writes=["mtmp0", "mtmp1", "hT"])

                if stop == "C1" and l == stop_l:
                    P.end()
                    P.close()
                    return nc
                def mix_consume(tb, coff, ncb, ps, pkey):
                    bi = coff // WC
                    P.op("act", lambda e, tb=tb: e.copy(out=mix[:, tb, coff:coff + ncb], in_=ps), reads=[pkey], writes=["mix"])
                lin_tm(w_out, w_out[l], 16, 0, D, hT, "hT", 4, mix_consume, wp)
                if stop == "C2" and l == stop_l:
                    P.end()
                    P.close()
                    return nc
                G1t, G1k = mrow(0)
                for tb in range(4):
                    j = tb % 2
                    xk = "xs%d" % j
                    P.dma("sp", xs[j][:], x_src[t0 + tb * 128:t0 + (tb + 1) * 128, :], writes=[xk])
                    def fms(e, tb=tb):
                        e.memzero(rs[:, 1:2])
                        e.activation(out=tmpf[:], in_=mix[:, tb, :], func=AF.Square, accum_out=rs[:, 1:2])
                        return act_rstd(e, rs[:, 0:1], rs[:, 1:2], D)
                    P.op("act", fms, reads=["mix"], writes=["tmpf", "rs"])
                    def fx(e, j=j, tb=tb, G1t=G1t):
                        e.scalar_tensor_tensor(out=tmpf[:], in0=mix[:, tb, :], scalar=rs[:, 0:1], in1=G1t[:], op0=ALU.mult, op1=ALU.mult)
                        return e.tensor_tensor(out=xs[j][:], in0=xs[j][:], in1=tmpf[:], op=ALU.add)
                    P.op("dve", fx, reads=[xk, "rs", "mix", G1k], writes=["tmpf", xk])
                    P.dma("act", XMID[t0 + tb * 128:t0 + (tb + 1) * 128, :], xs[j][:], reads=[xk], writes=["XMID"])
                A2t, A2k = mrow(1)
                B2t, B2k = mrow(3)
                for tb in range(4):
                    j = tb % 2
                    xk, hk = "xs%d" % j, "htm%d" % j
                    P.dma("sp", xs[j][:], XMID[t0 + tb * 128:t0 + (tb + 1) * 128, :], reads=["XMID"], writes=[xk])
                    def fsq(e, j=j):
                        e.memzero(rs[:, 2:3])
                        e.activation(out=htm[j][:], in_=xs[j][:], func=AF.Square, accum_out=rs[:, 2:3])
                        return act_rstd(e, rs[:, 3:4], rs[:, 2:3], D)
                    P.op("act", fsq, reads=[xk], writes=[hk, "rs2"])
                    def fn2(e, j=j, A2t=A2t, B2t=B2t):
                        e.scalar_tensor_tensor(out=tmpf[:], in0=xs[j][:], scalar=rs[:, 3:4], in1=A2t[:], op0=ALU.mult, op1=ALU.mult)
                        return e.tensor_tensor(out=htm[j][:], in0=tmpf[:], in1=B2t[:], op=ALU.add)
                    P.op("dve", fn2, reads=[xk, "rs2", A2k, B2k], writes=["tmpf", hk])
                    transposes_to_hT(j, hk, tb)
                if stop == "C3" and l == stop_l:
                    P.end()
                    P.close()
                    return nc
                for mc in range(FF // 256):
                    sg = sgt[mc % 2]
                    sk = "sg%d_" % (mc % 2)
                    def gate_consume(mi, ps, pkey, sg=sg, sk=sk):
                        P.op("act", lambda e, mi=mi: e.activation(out=sg[:, mi, :], in_=ps, func=AF.Silu), reads=[pkey], writes=[sk + str(mi)])
                    def up_consume(mi, ps, pkey, sg=sg, mc=mc, sk=sk):
                        P.op("dve", lambda e, mi=mi: e.tensor_tensor(out=actT[:, mc * 2 + mi, :], in0=ps, in1=sg[:, mi, :], op=ALU.mult),
                             reads=[pkey, sk + str(mi)], writes=["actT"])
                    lin_fm(w_ffn_in, w_ffn_in[l], 16, mc * 256, 256, hT, "hT", 512, gate_consume, wp, bank0=0)
                    lin_fm(w_ffn_in, w_ffn_in[l], 16, FF + mc * 256, 256, hT, "hT", 512, up_consume, wp, bank0=2)
                if stop == "C4" and l == stop_l:
                    P.end()
                    P.close()
                    return nc
                lin_tm(w_ffn_out, w_ffn_out[l], 44, 0, D, actT, "actT", 4, mix_consume, wp)
                G2t, G2k = mrow(2)
                for tb in range(4):
                    j = tb % 2
                    xk = "xs%d" % j
                    P.dma("sp", xs[j][:], XMID[t0 + tb * 128:t0 + (tb + 1) * 128, :], reads=["XMID"], writes=[xk])
                    def fms(e, tb=tb):
                        e.memzero(rs[:, 1:2])
                        e.activation(out=tmpf[:], in_=mix[:, tb, :], func=AF.Square, accum_out=rs[:, 1:2])
                        return act_rstd(e, rs[:, 0:1], rs[:, 1:2], D)
                    P.op("act", fms, reads=["mix"], writes=["tmpf", "rs"])
                    def fx2(e, j=j, tb=tb, G2t=G2t):
                        e.scalar_tensor_tensor(out=tmpf[:], in0=mix[:, tb, :], scalar=rs[:, 0:1], in1=G2t[:], op0=ALU.mult, op1=ALU.mult)
                        return e.tensor_tensor(out=xs[j][:], in0=xs[j][:], in1=tmpf[:], op=ALU.add)
                    P.op("dve", fx2, reads=[xk, "rs", "mix", G2k], writes=["tmpf", xk])
                    P.dma("act", x_dst[t0 + tb * 128:t0 + (tb + 1) * 128, :], xs[j][:], reads=[xk], writes=["XOUT"])
            P.end()

    P.close()
    return nc


_CACHE = {}


def _consts():
    rows = np.repeat(np.arange(TS // 64), 64)
    cols = np.tile(np.arange(64), TS // 64)
    inv = (10000.0 ** (-np.arange(32, dtype=np.float32) / 32)).astype(np.float32)
    ang = np.stack([rows, cols], axis=-1).astype(np.float32)[:, :, None] * inv
    cosS = np.cos(ang).reshape(TS, 64).astype(np.float32)
    sinS = np.sin(ang).reshape(TS, 64).astype(np.float32)
    cosT = np.concatenate([cosS, np.ones((TP, 64), np.float32)], 0)
    sinT = np.concatenate([sinS, np.zeros((TP, 64), np.float32)], 0)
    t = np.arange(128)
    le = (t[None, :] <= t[:, None])
    maskF = np.where(le, 0.0, NEG).astype(np.float32)
    maskB = np.where(le.T, 0.0, NEG).astype(np.float32)
    triF = (t[:, None] <= t[None, :]).astype(np.float32)
    triB = (t[:, None] >= t[None, :]).astype(np.float32)
    cmask = np.stack([maskF, maskB, triF, triB], 0)
    return cosT, sinT, cmask


def kernel(**inp):
    f = lambda a: np.ascontiguousarray(np.asarray(a, dtype=np.float32))
    if "nc" not in _CACHE:
        _CACHE["nc"] = build_program()
    nc = _CACHE["nc"]
    cosT, sinT, cmask = _consts()
    shared = {k: f(inp[k]) for k in ("w_mod", "b_mod", "norm_pre1", "norm_post1", "norm_pre2", "norm_post2", "w_in",
                                      "q_norm", "k_norm", "mlstm_gate_bias", "mlstm_norm", "conv_w", "w_branch", "w_out",
                                      "w_ffn_in", "w_ffn_out")}
    shared.update(cosT=cosT, sinT=sinT, cmask=cmask)
    x_prompt, x_sample = f(inp["x_prompt"]), f(inp["x_sample"])
    in_maps = []
    for c in range(8):
        b = c % 2
        m = dict(shared)
        m["x_s"] = x_sample[b]
        m["x_p"] = x_prompt[2 * c:2 * c + 2].reshape(TP, D)
        m["cache_k"] = f(inp["cache_k"])[b].reshape(DEPTH, NCTX, 256)
        m["cache_v"] = f(inp["cache_v"])[b].reshape(DEPTH, NCTX, 256)
        m["state_C"] = f(inp["state_C"])[b]
        m["state_n"] = f(inp["state_n"])[b]
        m["state_m"] = f(inp["state_m"])[b].reshape(DEPTH, 8)
        m["cond"] = np.stack([f(inp["c_ctx"]), f(inp["c"])[b]], 0)
        in_maps.append(m)
    res = run_bass_kernel_spmd(nc, in_maps, core_ids=list(range(8)))
    R = res.results
    y_prompt = np.concatenate([R[c]["y_p"].reshape(2, 256, D) for c in range(8)], 0)
    y_sample = np.stack([R[0]["y_s"], R[1]["y_s"]], 0)
    nk = np.concatenate([R[c]["o_k"].reshape(DEPTH, 2, 256, 2, 128).transpose(1, 0, 2, 3, 4) for c in range(8)], 0)
    nv = np.concatenate([R[c]["o_v"].reshape(DEPTH, 2, 256, 2, 128).transpose(1, 0, 2, 3, 4) for c in range(8)], 0)
    nC = np.concatenate([R[c]["o_C"] for c in range(8)], 0)
    nn = np.concatenate([R[c]["o_n"] for c in range(8)], 0)
    nm = np.concatenate([R[c]["o_m"].reshape(2, DEPTH, 2, 4) for c in range(8)], 0)
    return (y_prompt.astype(np.float32), y_sample.astype(np.float32), np.ascontiguousarray(nk), np.ascontiguousarray(nv),
            nC, nn, nm)
```
